# Optimizing a Trainium2 kernel written in Bass

```python
import math
import jax, jax.numpy as jnp
from jax import lax
import numpy as np

D_MODEL = 1024
BATCH = 8
SEQ = 2048
DEPTH = 2

N_HEADS = 16
HEAD_DIM = D_MODEL // N_HEADS
D_FF = ((8 * D_MODEL // 3 + 255) // 256) * 256
CONV_WIDTH = 3
N_MIXERS = 2
MOBA_BLOCK = 256
MOBA_TOPK = 3
MOBA_QCHUNK = 32
SB_QBLOCK = 128
REL_BUCKETS = 32
REL_MAX_DIST = 128
NORM_EPS = 1e-6
NEG = -1e30

kernel_name = "moba_stickbreaking_convffn_hybrid"


def rms_norm(x, g):
    xf = x.astype(jnp.float32)
    y = xf * lax.rsqrt(jnp.mean(xf * xf, axis=-1, keepdims=True) + NORM_EPS)
    return (y * g.astype(jnp.float32)).astype(x.dtype)


def rel_bucket(dist):
    n = jnp.maximum(dist, 0)
    max_exact = REL_BUCKETS // 2
    nf = jnp.maximum(n, 1).astype(jnp.float32)
    large = max_exact + (jnp.log(nf / max_exact) / math.log(REL_MAX_DIST / max_exact)
                         * (REL_BUCKETS - max_exact)).astype(jnp.int32)
    large = jnp.minimum(large, REL_BUCKETS - 1)
    return jnp.where(n < max_exact, n, large)


def moba_attention(q, k, v, rel_bias):
    B, H, S, Dh = q.shape
    BS = MOBA_BLOCK
    QC = MOBA_QCHUNK
    nb = -(-S // BS)
    pad = nb * BS - S
    kp = jnp.pad(k, ((0, 0), (0, 0), (0, pad), (0, 0))).reshape(B, H, nb, BS, Dh)
    vp = jnp.pad(v, ((0, 0), (0, 0), (0, pad), (0, 0))).reshape(B, H, nb, BS, Dh)
    scale = Dh ** -0.5

    pos = jnp.arange(S)
    qblk = pos // BS
    kmean = jnp.mean(kp.astype(jnp.float32), axis=3)
    gate = jnp.einsum('bhsd,bhnd->bhsn', q.astype(jnp.float32), kmean)
    past = jnp.arange(nb)[None, :] < qblk[:, None]
    gate = jnp.where(past, gate, NEG)
    kk = min(MOBA_TOPK, nb)
    _, top_idx = lax.top_k(gate, kk)
    valid = jnp.arange(kk)[None, :] < qblk[:, None]

    nc = S // QC
    q_c = q.reshape(B, H, nc, QC, Dh).transpose(2, 0, 1, 3, 4)
    idx_c = top_idx.reshape(B, H, nc, QC, kk).transpose(2, 0, 1, 3, 4)
    valid_c = valid.reshape(nc, QC, kk)
    bi = jnp.arange(B)[:, None, None]
    hi = jnp.arange(H)[None, :, None]
    hi4 = jnp.arange(H)[None, :, None, None]
    offs = jnp.arange(BS)

    def chunk(args):
        ci, qi, idx, vld = args
        qpos = ci * QC + jnp.arange(QC)
        own = (ci * QC) // BS
        k_own = lax.dynamic_index_in_dim(kp, own, axis=2, keepdims=False)
        v_own = lax.dynamic_index_in_dim(vp, own, axis=2, keepdims=False)
        kpos_own = own * BS + offs
        s_own = jnp.einsum('bhqd,bhkd->bhqk', qi, k_own).astype(jnp.float32) * scale
        s_own = s_own + rel_bias[:, rel_bucket(qpos[:, None] - kpos_own[None, :])].astype(jnp.float32)
        s_own = jnp.where(kpos_own[None, :] <= qpos[:, None], s_own, NEG)
        logits = []
        for j in range(kk):
            blk = idx[..., j]
            kg = kp[bi, hi, blk]
            s = jnp.einsum('bhqd,bhqkd->bhqk', qi, kg).astype(jnp.float32) * scale
            kpos = blk[..., None] * BS + offs
            bias = rel_bias[hi4, rel_bucket(qpos[None, None, :, None] - kpos)].astype(jnp.float32)
            logits.append(jnp.where(vld[None, None, :, j, None], s + bias, NEG))
        logits.append(s_own)
        p = jax.nn.softmax(jnp.concatenate(logits, axis=-1), axis=-1).astype(v.dtype)
        out = jnp.einsum('bhqk,bhkd->bhqd', p[..., kk * BS:], v_own)
        for j in range(kk):
            vg = vp[bi, hi, idx[..., j]]
            out = out + jnp.einsum('bhqk,bhqkd->bhqd', p[..., j * BS:(j + 1) * BS], vg)
        return out

    out = lax.map(chunk, (jnp.arange(nc), q_c, idx_c, valid_c))
    return out.transpose(1, 2, 0, 3, 4).reshape(B, H, S, Dh)


def stick_breaking_attention(q, k, v):
    B, H, S, Dh = q.shape
    QB = SB_QBLOCK
    nq = S // QB
    scale = Dh ** -0.5
    kpos = jnp.arange(S)
    q_b = q.reshape(B, H, nq, QB, Dh).transpose(2, 0, 1, 3, 4)

    def block(args):
        bidx, qi = args
        qpos = bidx * QB + jnp.arange(QB)
        z = jnp.einsum('bhqd,bhkd->bhqk', qi, k).astype(jnp.float32) * scale
        strict = kpos[None, :] < qpos[:, None]
        log_beta = jax.nn.log_sigmoid(z)
        log_1mb = jnp.where(strict, jax.nn.log_sigmoid(-z), 0.0)
        tail = lax.cumsum(log_1mb, axis=3, reverse=True) - log_1mb
        a = jnp.where(strict, jnp.exp(log_beta + tail), 0.0)
        return jnp.einsum('bhqk,bhkd->bhqd', a.astype(v.dtype), v)

    out = lax.map(block, (jnp.arange(nq), q_b))
    return out.transpose(1, 2, 0, 3, 4).reshape(B, H, S, Dh)


def conv_ffn(y, w_up, conv_w, conv_b, w_down):
    u = y @ w_up
    c = u.shape[-1]
    u = lax.conv_general_dilated(u, conv_w[:, None, :], window_strides=(1,),
                                 padding=[(CONV_WIDTH - 1, 0)],
                                 dimension_numbers=('NWC', 'WIO', 'NWC'),
                                 feature_group_count=c) + conv_b
    gate, val = jnp.split(u, 2, axis=-1)
    return (jax.nn.silu(gate) * val) @ w_down


def setup_inputs(seed: int = 0) -> dict:
    key = jax.random.key(seed)
    ks = jax.random.split(key, 12)
    D, F = D_MODEL, D_FF
    f32 = jnp.float32
    return {
        "x": jax.random.normal(ks[0], (BATCH, SEQ, D), f32),
        "attn_norm": 1.0 + 0.02 * jax.random.normal(ks[1], (DEPTH, D), f32),
        "w_qkv": jax.random.normal(ks[2], (DEPTH, D, 3 * D), f32) * D ** -0.5,
        "w_o": jax.random.normal(ks[3], (DEPTH, D, D), f32) * D ** -0.5,
        "rel_bias": 0.5 * jax.random.normal(ks[4], (N_HEADS, REL_BUCKETS), f32),
        "ffn_norm": 1.0 + 0.02 * jax.random.normal(ks[5], (DEPTH, D), f32),
        "w_up": jax.random.normal(ks[6], (DEPTH, D, 2 * F), f32) * D ** -0.5,
        "conv_w": jax.random.normal(ks[7], (DEPTH, CONV_WIDTH, 2 * F), f32) * CONV_WIDTH ** -0.5,
        "conv_b": 0.02 * jax.random.normal(ks[8], (DEPTH, 2 * F), f32),
        "w_down": jax.random.normal(ks[9], (DEPTH, F, D), f32) * F ** -0.5,
        "final_norm": 1.0 + 0.02 * jax.random.normal(ks[10], (D,), f32),
    }


def reference(x, attn_norm, w_qkv, w_o, rel_bias, ffn_norm, w_up, conv_w, conv_b, w_down, final_norm):
    B, S, D = x.shape
    h = x
    for i in range(DEPTH):
        y = rms_norm(h, attn_norm[i])
        qkv = (y @ w_qkv[i]).reshape(B, S, 3, N_HEADS, HEAD_DIM)
        q = qkv[:, :, 0].transpose(0, 2, 1, 3)
        k = qkv[:, :, 1].transpose(0, 2, 1, 3)
        v = qkv[:, :, 2].transpose(0, 2, 1, 3)
        if i % N_MIXERS == 0:
            o = moba_attention(q, k, v, rel_bias)
        else:
            o = stick_breaking_attention(q, k, v)
        h = h + o.transpose(0, 2, 1, 3).reshape(B, S, D) @ w_o[i]
        y = rms_norm(h, ffn_norm[i])
        h = h + conv_ffn(y, w_up[i], conv_w[i], conv_b[i], w_down[i])
    return rms_norm(h, final_norm)
```

```python
import contextlib
import numpy as np
import ml_dtypes
import concourse.bass as bass
import concourse.mybir as mybir
from concourse.bass_utils import run_bass_kernel_spmd

F32 = mybir.dt.float32
BF16 = mybir.dt.bfloat16
F32R = mybir.dt.float32r
AF = mybir.ActivationFunctionType
ALU = mybir.AluOpType
AX = mybir.AxisListType

S = 2048
D = 1024
DFF = 2816
NEGV = -30000.0
COMPUTE = ("pe", "act", "dve", "pool")
QUEUES = ("sp", "pool")
NDMASEM = 12


class T:
    __slots__ = ("w", "rs")

    def __init__(self):
        self.w = None
        self.rs = []


class TP(T):
    __slots__ = ()


def _flat(xs):
    out = []
    for x in xs:
        if isinstance(x, T):
            out.append(x)
        elif isinstance(x, (list, tuple)):
            out.extend(_flat(x))
        else:
            out.extend(x.ts)
    return out


class Prog:
    def __init__(self):
        self.ops = []
        self.dma_rr = {q: 0 for q in QUEUES}
        self.dma_tot = {}

    def op(self, eng, fn, r=(), w=(), dma=False):
        i = len(self.ops)
        r = _flat(r)
        w = _flat(w)
        w = w + [b for b in r if isinstance(b, TP)]
        r = [b for b in r if not isinstance(b, TP)]
        deps = {}
        for b in r:
            if b.w is not None:
                deps[b.w] = "raw"
        for b in w:
            if b.w is not None:
                deps.setdefault(b.w, "waw")
            for x in b.rs:
                deps.setdefault(x, "war")
        for b in r:
            b.rs.append(i)
        for b in w:
            b.w = i
            b.rs = []
        deps.pop(i, None)
        o = dict(eng=eng, fn=fn, deps=deps, dma=dma, sig=False, cnt=None, semi=None, semv=None, prev=None)
        if dma:
            k = self.dma_rr[eng]
            self.dma_rr[eng] = (k + 1) % NDMASEM
            key = (eng, k)
            o["prev"] = self.dma_tot.get(key, 0)
            self.dma_tot[key] = o["prev"] + 16
            o["semi"] = key
            o["semv"] = o["prev"] + 16
        self.ops.append(o)
        return i

    def finalize(self):
        ops = self.ops
        for o in ops:
            need = {}
            for d, kind in o["deps"].items():
                p = ops[d]
                if p["dma"]:
                    need[d] = kind
                    continue
                if p["eng"] == o["eng"] and not o["dma"]:
                    if o["eng"] == "pe" or kind != "raw":
                        continue
                need[d] = kind
                p["sig"] = True
            o["need"] = need
        cnt = {e: 0 for e in set(COMPUTE + QUEUES)}
        for o in ops:
            if o["sig"] and not o["dma"]:
                cnt[o["eng"]] += 1
                o["cnt"] = cnt[o["eng"]]

    def emit(self, block, sems, dsems, out_final):
        ops = self.ops
        engs = {"pe": block.tensor, "act": block.scalar, "dve": block.vector, "pool": block.gpsimd, "sp": block.sync}
        for ename, deco in engs.items():
            mine = [o for o in ops if o["eng"] == ename]

            def body(e, mine=mine, ename=ename):
                waited = {}
                for o in mine:
                    for d in o["need"]:
                        p = ops[d]
                        if p["dma"]:
                            key, v, s = ("d",) + p["semi"], p["semv"], dsems[p["semi"]]
                        else:
                            key, v, s = p["eng"], p["cnt"], sems[p["eng"]]
                        if waited.get(key, 0) >= v:
                            continue
                        waited[key] = v
                        e.wait_ge(s, v)
                    if o["dma"]:
                        key = ("d",) + o["semi"]
                        if o["prev"] > 0 and waited.get(key, 0) < o["prev"]:
                            e.wait_ge(dsems[o["semi"]], o["prev"])
                            waited[key] = o["prev"]
                    ins = o["fn"](e)
                    if o["dma"]:
                        ins.then_inc(dsems[o["semi"]], 16)
                    elif o["sig"]:
                        ins.then_inc(sems[o["eng"]], 1)
                if ename == "sp":
                    for d in out_final:
                        p = ops[d]
                        e.wait_ge(dsems[p["semi"]], p["semv"])

            deco(body)


class Buf:
    def __init__(self, ap, ts, off, slot, esz):
        self.ap = ap
        self.ts = ts
        self.off = off
        self.slot = slot
        self.esz = esz

    def t(self, lo, hi):
        a = (self.off + lo * self.esz) // self.slot
        b = (self.off + hi * self.esz - 1) // self.slot
        base = self.off // self.slot
        return self.ts[a - base:b - base + 1]


class Arena:
    SLOT = 512

    def __init__(self, ap_bf16, nbytes):
        self.ap = ap_bf16
        self.n = nbytes
        self.T = [T() for _ in range(nbytes // self.SLOT)]
        self.off = 0

    def alloc(self, nelem, dt=BF16, parts=128, shape=None):
        esz = 4 if dt in (F32, F32R) else 2
        nb = nelem * esz
        nb = (nb + self.SLOT - 1) // self.SLOT * self.SLOT
        off = self.off
        self.off += nb
        assert self.off <= self.n, ("arena overflow", self.off, self.n)
        v = self.ap[0:parts, off // 2:(off + nelem * esz) // 2]
        if dt != BF16:
            v = v.bitcast(dt)
        if shape is not None:
            names = " ".join("abcdef"[i] for i in range(len(shape)))
            v = v.rearrange("p (%s) -> p %s" % (names, names), **{"abcdef"[i]: shape[i] for i in range(len(shape))})
        ts = self.T[off // self.SLOT:(off + nb) // self.SLOT]
        return Buf(v, ts, off, self.SLOT, esz)


def build(nlayers=2, stop_after=None):
    nc = bass.Bass("TRN2", target_bir_lowering=False)

    def DI(name, shape, dt=F32):
        return nc.dram_tensor(name, shape, dt, kind="ExternalInput").ap()

    xT_d = DI("xT", [D, S])
    gains_d = DI("gains", [128, 40])
    wqkv_d = DI("wqkv", [2, 8, 128, 8 * 384])
    wo_d = DI("wo", [2, D, D])
    wup_d = DI("wup", [2, 6, 128, 8 * 1024])
    wdn_d = DI("wdn", [2, DFF, D])
    convp_d = DI("convp", [128, 2 * 4 * 44])
    relT_d = DI("relT", [32, 16])
    idb_d = DI("idb", [128, 128], BF16)
    U_d = DI("Umat", [128, 128], BF16)
    L_d = DI("Lmat", [128, 128], BF16)
    Mr_d = DI("Mr", [128, 896], BF16)
    koh_d = DI("koh", [8, S], BF16)
    gmask_d = DI("gmask", [128, 128])
    outT_d = nc.dram_tensor("outT", [D, S], F32, kind="ExternalOutput").ap()
    gscr_d = DI("gtab", [16, 128, 256])

    P = Prog()
    ARENA_BYTES = 100 * 1024
    es = contextlib.ExitStack()
    with es:
        hT = es.enter_context(nc.sbuf_tensor("hT", [128, 8, S], F32))
        yT = es.enter_context(nc.sbuf_tensor("yT", [128, 8, S + 4], BF16))
        arena_t = es.enter_context(nc.sbuf_tensor("arena", [128, ARENA_BYTES // 2], BF16))
        gains = es.enter_context(nc.sbuf_tensor("gains_sb", [128, 40], F32))
        convp = es.enter_context(nc.sbuf_tensor("convp_sb", [128, 2, 4, 44], F32))
        idb = es.enter_context(nc.sbuf_tensor("idb_sb", [128, 128], BF16))
        Um = es.enter_context(nc.sbuf_tensor("U_sb", [128, 128], BF16))
        Lm = es.enter_context(nc.sbuf_tensor("L_sb", [128, 128], BF16))
        Mx = es.enter_context(nc.sbuf_tensor("Mr_sb", [128, 896], BF16))
        onesf = es.enter_context(nc.sbuf_tensor("onesf", [128, 128], F32))
        onesr = es.enter_context(nc.sbuf_tensor("onesr", [128, 128], F32R))
        onesb = es.enter_context(nc.sbuf_tensor("onesb", [128, 64], BF16))
        sqr = es.enter_context(nc.sbuf_tensor("sqr", [128, 2, 512], F32R))
        sqT = [T(), T()]
        cb = es.enter_context(nc.sbuf_tensor("cb", [128, 16], F32))
        kms = es.enter_context(nc.sbuf_tensor("kms", [128, 8], F32))
        kmb = es.enter_context(nc.sbuf_tensor("kmb", [128, 8], BF16))
        ps = es.enter_context(nc.psum_tensor("ps", [128, 8, 512], F32))
        psb = ps[:, 7, :].bitcast(BF16)

        arena = Arena(arena_t[:], ARENA_BYTES)
        hT_T = [[T() for _ in range(4)] for _ in range(8)]
        yT_T = [[T() for _ in range(4)] for _ in range(8)]
        bankQ = [[TP(), TP()] for _ in range(8)]

        def BK(b, half=None, rows=None):
            rs = [0, 1] if rows is None else [rows]
            return [bankQ[b][r] for r in rs]

        class _BT:
            def __getitem__(self, b):
                return BK(b)
        bankT = _BT()
        cT = {k: T() for k in ("gains", "convp", "idb", "U", "L", "Mr", "ones", "cb", "relT", "onehot", "negm", "Fsb", "fscr",
                               "g8", "top8", "npad0", "npad1", "kms", "kmb", "ypad")}

        def MM(out, lhsT, rhs, start, stop, r, w):
            P.op("pe", lambda e: e.matmul(out, lhsT=lhsT, rhs=rhs, start=start, stop=stop), r=r, w=w)

        def ACT(out, in_, func, r, w, **kw):
            P.op("act", lambda e: e.activation(out=out, in_=in_, func=func, **kw), r=r, w=w)

        def TT(out, in0, in1, op, r, w):
            P.op("dve", lambda e: e.tensor_tensor(out=out, in0=in0, in1=in1, op=op), r=r, w=w)

        def TS(out, in0, s1, s2, op0, op1, r, w):
            P.op("dve", lambda e: e.tensor_scalar(out=out, in0=in0, scalar1=s1, scalar2=s2, op0=op0, op1=op1), r=r, w=w)

        def STT(out, in0, scalar, in1, op0, op1, r, w):
            P.op("dve", lambda e: e.scalar_tensor_tensor(out=out, in0=in0, scalar=scalar, in1=in1, op0=op0, op1=op1), r=r, w=w)

        def PTT(out, in0, in1, op, r, w):
            P.op("pool", lambda e: e.tensor_tensor(out=out, in0=in0, in1=in1, op=op), r=r, w=w)

        def VCOPY(out, in_, r, w):
            P.op("dve", lambda e: e.tensor_copy(out=out, in_=in_), r=r, w=w)

        def VOP(fn, r, w):
            P.op("dve", fn, r=r, w=w)

        def DMA(q, out, in_, r, w, **kw):
            return P.op(q, lambda e: e.dma_start(out=out, in_=in_, **kw), r=r, w=w, dma=True)

        DMA("sp", gains[:], gains_d[:], [], [cT["gains"]])
        DMA("sp", convp[:].rearrange("p a b c -> p (a b c)"), convp_d[:], [], [cT["convp"]])
        DMA("sp", idb[:], idb_d[:], [], [cT["idb"]])
        DMA("sp", Um[:], U_d[:], [], [cT["U"]])
        DMA("sp", Lm[:], L_d[:], [], [cT["L"]])
        DMA("sp", Mx[:], Mr_d[:], [], [cT["Mr"]])
        DMA("sp", cb[:], relT_d[31:32, :].partition_broadcast(128), [], [cT["cb"]])
        for tc in range(4):
            for c in range(8):
                DMA("sp", hT[:, c, tc * 512:(tc + 1) * 512], xT_d[c * 128:(c + 1) * 128, tc * 512:(tc + 1) * 512], [], [hT_T[c][tc]])
        VOP(lambda e: e.memset(onesf[:], 1.0), [], [cT["ones"]])
        VCOPY(onesr[:], onesf[:], [cT["ones"]], [cT["ones"]])
        VCOPY(onesb[:], onesf[:, 0:64], [cT["ones"]], [cT["ones"]])
        VOP(lambda e: e.memset(yT[:, :, 0:2], 0.0), [], [cT["ypad"]])
        out_final = []

        def norm(gi, final=False):
            arena.off = 0
            lnt = [arena.alloc(512, F32) for _ in range(2)]
            stg = [arena.alloc(512, F32) for _ in range(3)] if final else None
            k = 0
            for tc in range(4):
                sl = slice(tc * 512, (tc + 1) * 512)
                b = 5 + (tc % 2)
                for c in range(8):
                    sq = sqT[k % 2]
                    sqa = sqr[:, k % 2, :]
                    k += 1
                    ACT(sqa, hT[:, c, sl], AF.Square, [hT_T[c][tc]], [sq])
                    MM(ps[:, b, :], onesr[:], sqa, c == 0, c == 7, [sq, cT["ones"]], [bankT[b]])
                lt = lnt[tc % 2]
                ACT(lt.ap, ps[:, b, :], AF.Ln, [bankT[b]], [lt], scale=1.0 / D, bias=1e-6)
                ACT(ps[:, b, :], lt.ap, AF.Exp, [lt], [bankT[b]], scale=-0.5)
                for c in range(8):
                    if not final:
                        STT(yT[:, c, 2 + tc * 512:2 + (tc + 1) * 512], hT[:, c, sl], gains[:, gi * 8 + c:gi * 8 + c + 1], ps[:, b, :],
                            ALU.mult, ALU.mult, [hT_T[c][tc], cT["gains"], bankT[b], cT["ypad"]], [yT_T[c][tc]])
                    else:
                        st = stg[(tc * 8 + c) % 3]
                        STT(st.ap, hT[:, c, sl], gains[:, gi * 8 + c:gi * 8 + c + 1], ps[:, b, :],
                            ALU.mult, ALU.mult, [hT_T[c][tc], cT["gains"], bankT[b]], [st])
                        out_final.append(DMA("sp", outT_d[c * 128:(c + 1) * 128, sl], st.ap, [st], []))

        def dump_h():
            for c in range(8):
                out_final.append(DMA("sp", outT_d[c * 128:(c + 1) * 128, :], hT[:, c, :], [hT_T[c]], []))

        def yrhs(c, lo, n):
            return yT[:, c, lo:lo + n]

        def yts(c, lo, hi):
            a = max(0, (lo - 2)) // 512
            b = min(3, max(0, hi - 3) // 512)
            return [yT_T[c][i] for i in range(a, b + 1)]

        def attention(l):
            moba = (l % 2 == 0)
            arena.off = 0
            sets = []
            for s_ in range(2):
                d = {}
                d["q"] = arena.alloc(S, BF16)
                d["k"] = arena.alloc(S, BF16)
                if moba:
                    d["qx"] = arena.alloc(S, BF16)
                    d["kx"] = arena.alloc(S, BF16)
                    d["Tb"] = arena.alloc(4 * 128, F32, shape=[2, 2, 128])
                d["v"] = arena.alloc(16 * 128, BF16, shape=[16, 128])
                d["vT"] = arena.alloc(S, BF16)
                d["o"] = arena.alloc(S, BF16)
                d["wq"] = arena.alloc(8 * 384, BF16, shape=[8, 384])
                sets.append(d)
            wos = [arena.alloc(D, BF16) for _ in range(4)]
            if moba:
                Pt = [arena.alloc(512, BF16) for _ in range(6)]
                g16b = arena.alloc(128, F32, shape=[16, 8])
                top16b = arena.alloc(128, F32, shape=[16, 8])
                npad16b = arena.alloc(16 * 72, BF16, shape=[16, 72])
                gmaskb = arena.alloc(128, F32)
                g16, top16, npad16, gmask = g16b.ap, top16b.ap, npad16b.ap, gmaskb.ap
                VOP(lambda e: e.memset(npad16, 0.0), [], [npad16b])
                DMA("sp", gmask, gmask_d[:], [], [gmaskb])
                tmpb = [arena.alloc(128, F32) for _ in range(4)]
                rden = [arena.alloc(512, F32) for _ in range(2)]
            else:
                eb = [arena.alloc(512, F32) for _ in range(4)]
                spm = [arena.alloc(512, F32) for _ in range(4)]
                spp = [arena.alloc(512, BF16) for _ in range(4)]
                wb = [arena.alloc(512, F32) for _ in range(4)]
                ab = [arena.alloc(512, BF16) for _ in range(4)]

            def load_weights(p):
                d = sets[p % 2]
                DMA("pool", d["wq"].ap, wqkv_d[l, p].rearrange("r (c n) -> r c n", c=8), [], [d["wq"]], max_dma_last_dim=4096)
                DMA("pool", wos[p % 4].ap, wo_d[l, p * 128:(p + 1) * 128, :], [], [wos[p % 4]], max_dma_last_dim=4096)
                if moba:
                    for h2 in range(2):
                        DMA("sp", d["kx"].ap[h2 * 64:h2 * 64 + 8, :], koh_d[:, :], [], [d["kx"]])
                        h = 2 * p + h2
                        DMA("sp", d["Tb"].ap[:, h2, :, :], gscr_d[h].rearrange("k (w q) -> k w q", w=2), [], [d["Tb"]])

            pj_rot = [6, 0, 1] if moba else [0, 4, 3]
            pjc = [0]

            def pjbank():
                b = pj_rot[pjc[0] % 3]
                pjc[0] += 1
                return b

            def project(p):
                d = sets[p % 2]
                wq = d["wq"]
                for tc in range(4):
                    sl = slice(tc * 512, (tc + 1) * 512)
                    b = pjbank()
                    for c in range(8):
                        MM(ps[:, b, :], wq.ap[:, c, 0:128], yrhs(c, 2 + tc * 512, 512), c == 0, c == 7, [wq, yT_T[c][tc]], [bankT[b]])
                    ACT(d["q"].ap[:, sl], ps[:, b, :], AF.Copy, [bankT[b]], [d["q"].t(tc * 512, tc * 512 + 512)], scale=0.125)
                    b = pjbank()
                    for c in range(8):
                        MM(ps[:, b, :], wq.ap[:, c, 128:256], yrhs(c, 2 + tc * 512, 512), c == 0, c == 7, [wq, yT_T[c][tc]], [bankT[b]])
                    if moba:
                        for hf in range(2):
                            cs = slice(tc * 512 + hf * 256, tc * 512 + hf * 256 + 256)
                            ACT(d["k"].ap[:, cs], ps[:, b, hf * 256:hf * 256 + 256], AF.Copy, [bankT[b]],
                                [d["k"].t(cs.start, cs.stop), cT["kms"]], accum_out=kms[:, tc * 2 + hf:tc * 2 + hf + 1])
                    else:
                        VCOPY(d["k"].ap[:, sl], ps[:, b, :], [bankT[b]], [d["k"].t(tc * 512, tc * 512 + 512)])
                if moba:
                    ACT(kmb[:, :], kms[:, :], AF.Copy, [cT["kms"]], [cT["kmb"]], scale=1.0 / 256)
                vT = d["vT"]
                for tc in range(4):
                    sl = slice(tc * 512, (tc + 1) * 512)
                    b = pjbank()
                    for c in range(8):
                        MM(ps[:, b, :], wq.ap[:, c, 256:384], yrhs(c, 2 + tc * 512, 512), c == 0, c == 7, [wq, yT_T[c][tc]], [bankT[b]])
                    ACT(vT.ap[:, sl], ps[:, b, :], AF.Copy, [bankT[b]], [vT.t(tc * 512, tc * 512 + 512)])
                    pb = psb
                    for j in range(4):
                        tt = tc * 4 + j
                        P.op("pe", lambda e, pb=pb, j=j, tt=tt: e.transpose(out=pb[:, j * 128:(j + 1) * 128], in_=vT.ap[:, tt * 128:(tt + 1) * 128], identity=idb[:]),
                             r=[vT.t(tt * 128, tt * 128 + 128), cT["idb"]], w=[bankT[7]])
                    VCOPY(d["v"].ap[:, tc * 4:(tc + 1) * 4, :], pb[:, 0:512].rearrange("p (a b) -> p a b", a=4), [bankT[7]],
                          [d["v"].t(tc * 512, tc * 512 + 512)])

            def out_proj2(p0):
                for tc in range(4):
                    for dt_ in range(8):
                        sl = slice(tc * 512, (tc + 1) * 512)
                        b = pjbank()
                        for k_, p_ in enumerate((p0, p0 + 1)):
                            d = sets[p_ % 2]
                            MM(ps[:, b, :], wos[p_ % 4].ap[:, dt_ * 128:(dt_ + 1) * 128], d["o"].ap[:, sl], k_ == 0, k_ == 1,
                               [wos[p_ % 4], d["o"].t(tc * 512, tc * 512 + 512)], [bankT[b]])
                        TT(hT[:, dt_, sl], ps[:, b, :], hT[:, dt_, sl], ALU.add, [bankT[b], hT_T[dt_][tc]], [hT_T[dt_][tc]])


            def moba_select(p):
                d = sets[p % 2]
                for h2 in range(2):
                    R = slice(h2 * 64, h2 * 64 + 64)
                    for j in range(8):
                        qt = 8 + j
                        c = (h2 * 8 + j) * 8
                        MM(ps[:, 6, c:c + 8], d["q"].ap[R, qt * 128:(qt + 1) * 128], kmb[R, :], True, True,
                           [d["q"].t(qt * 128, qt * 128 + 128), cT["kmb"]], [bankT[6]])
                TT(g16.rearrange("p a b -> p (a b)"), ps[:, 6, 0:128], gmask, ALU.add, [bankT[6], gmaskb], [g16b])
                for s_ in range(16):
                    VOP(lambda e, s_=s_: e.max(out=top16[:, s_, :], in_=g16[:, s_, :]), [g16b], [top16b])
                for s_ in range(16):
                    h2 = s_ // 8
                    TS(npad16[:, s_, h2 * 64:h2 * 64 + 8], g16[:, s_, :], top16[:, s_, 2:3], NEGV, ALU.is_lt, ALU.mult,
                       [g16b, top16b], [npad16b])
                for h2 in range(2):
                    for j in range(8):
                        P.op("pe", lambda e, h2=h2, j=j: e.transpose(out=psb[0:72, j * 128:(j + 1) * 128], in_=npad16[:, h2 * 8 + j, :], identity=idb[:]),
                             r=[npad16b, cT["idb"]], w=[BK(7)])
                    ACT(d["qx"].ap[h2 * 64:h2 * 64 + 8, 1024:2048], psb[h2 * 64:h2 * 64 + 8, 0:1024], AF.Copy, [BK(7)],
                        [d["qx"].t(1024, 2048)])

            def moba_core(p):
                d = sets[p % 2]
                items = []
                for C in range(4):
                    base = 512 * C
                    its = []
                    for kt in range(4 * C):
                        its.append((kt, base, 512))
                    its.append((4 * C, base, 512))
                    its.append((4 * C + 1, base + 128, 384))
                    its.append((4 * C + 2, base + 256, 256))
                    its.append((4 * C + 3, base + 384, 128))
                    for ii, (kt, q0, W) in enumerate(its):
                        items.append((C, kt, q0, W, ii == 0, ii == len(its) - 1))
                sc = [0]
                SBK = ((0, 1), (6, 7))

                def stageA(it, idx):
                    C, kt, q0, W, first, last = it
                    n = kt // 2
                    c0 = q0 - 512 * C
                    a0 = max(q0, 256 * (n + 1), 1024)
                    for h2 in range(2):
                        R = slice(h2 * 64, h2 * 64 + 64)
                        sb_ = SBK[idx % 2][h2]
                        MM(ps[:, sb_, c0:c0 + W], d["k"].ap[R, kt * 128:(kt + 1) * 128], d["q"].ap[R, q0:q0 + W], True, True,
                           [d["k"].t(kt * 128, kt * 128 + 128), d["q"].t(q0, q0 + W)], [BK(sb_)])
                    if a0 < q0 + W:
                        ca = a0 - 512 * C
                        for h2 in range(2):
                            X = slice(h2 * 64, h2 * 64 + 8)
                            sb_ = SBK[idx % 2][h2]
                            MM(ps[:, sb_, ca:c0 + W], d["kx"].ap[X, kt * 128:(kt + 1) * 128], d["qx"].ap[X, a0:q0 + W], False, True,
                               [d["kx"].t(kt * 128, kt * 128 + 128), d["qx"].t(a0, q0 + W)], [BK(sb_)])

                def stageB(it, idx):
                    C, kt, q0, W, first, last = it
                    c0 = q0 - 512 * C
                    kinds = []
                    for u in range(W // 128):
                        delta = q0 + u * 128 - kt * 128
                        kinds.append(0 if delta == 0 else (1 if delta == 128 else 2))
                    groups = []
                    u = 0
                    while u < len(kinds):
                        if kinds[u] == 2:
                            v_ = u
                            while v_ < len(kinds) and kinds[v_] == 2:
                                v_ += 1
                            groups.append((2, u, v_))
                            u = v_
                        else:
                            groups.append((kinds[u], u, u + 1))
                            u += 1
                    for h2 in range(2):
                        h = 2 * p + h2
                        sb_ = SBK[idx % 2][h2]
                        pt = Pt[(idx % 3) * 2 + h2]
                        for kd, u0, u1 in groups:
                            cs = slice(c0 + u0 * 128, c0 + u1 * 128)
                            if kd == 2:
                                ACT(pt.ap[:, cs], ps[:, sb_, cs], AF.Exp, [BK(sb_), cT["cb"]], [pt], bias=cb[:, h:h + 1], scale=1.0)
                            else:
                                tb = tmpb[sc[0] % 4]
                                sc[0] += 1
                                TT(tb.ap, ps[:, sb_, cs], d["Tb"].ap[:, h2, kd, :], ALU.add, [BK(sb_), d["Tb"]], [tb])
                                ACT(pt.ap[:, cs], tb.ap, AF.Exp, [tb], [pt])

                def stageC(it, idx):
                    C, kt, q0, W, first, last = it
                    c0 = q0 - 512 * C
                    nb, db = 2 + C % 2, 4 + C % 2
                    for h2 in range(2):
                        pt = Pt[(idx % 3) * 2 + h2]
                        R = slice(h2 * 64, h2 * 64 + 64)
                        MM(ps[R, nb, c0:c0 + W], d["v"].ap[:, kt, h2 * 64:h2 * 64 + 64], pt.ap[:, c0:c0 + W], first, last,
                           [d["v"].t(kt * 128, kt * 128 + 128), pt], [BK(nb, 0, h2)])
                    for h2 in range(2):
                        pt = Pt[(idx % 3) * 2 + h2]
                        R = slice(h2 * 64, h2 * 64 + 64)
                        MM(ps[R, db, c0:c0 + W], onesb[:, 0:64], pt.ap[:, c0:c0 + W], first, last, [cT["ones"], pt], [BK(db, 0, h2)])
                    if last:
                        rd = rden[C % 2]
                        VOP(lambda e, db=db, rd=rd: e.reciprocal(out=rd.ap[:, :], in_=ps[:, db, :]), [BK(db)], [rd])
                        TT(d["o"].ap[:, 512 * C:512 * C + 512], ps[:, nb, :], rd.ap[:, :], ALU.mult,
                           [BK(nb), rd], [d["o"].t(512 * C, 512 * C + 512)])

                ni = len(items)
                stageA(items[0], 0)
                for i in range(ni + 1):
                    if i + 1 < ni:
                        stageA(items[i + 1], i + 1)
                    if 0 <= i - 1 < ni:
                        stageC(items[i - 1], i - 1)
                    if i < ni:
                        stageB(items[i], i)


            def sb_core(p):
                d = sets[p % 2]
                items = []
                for c4 in range(4):
                    nk = 4 * c4 + 4
                    for i in range(nk - 1, -1, -1):
                        items.append((c4, i, i == nk - 1, i == 0))
                dc = [0]
                ZB = ((0, 7), (3, 4))
                TBK = (1, 2)
                RR = (slice(0, 64), slice(64, 128))

                TRI = Mx[:, 384:512]

                def cz(it):
                    c4, i, first, last = it
                    r = i - 4 * c4
                    return 128 * max(r, 0), r

                def st1(it, n):
                    c4, i, first, last = it
                    c0, r = cz(it)
                    for h2 in range(2):
                        z = ZB[n % 2][h2]
                        MM(ps[:, z, c0:512], d["k"].ap[RR[h2], i * 128:(i + 1) * 128], d["q"].ap[RR[h2], c4 * 512 + c0:(c4 + 1) * 512], True, True,
                           [d["k"].t(i * 128, i * 128 + 128), d["q"].t(c4 * 512 + c0, c4 * 512 + 512)], [BK(z)])

                def st2a(it, n):
                    c4, i, first, last = it
                    c0, r = cz(it)
                    for h2 in range(2):
                        z = ZB[n % 2][h2]
                        e_ = eb[(n % 2) * 2 + h2]
                        m = (n % 2) * 2 + h2
                        ACT(e_.ap[:, c0:512], ps[:, z, c0:512], AF.Exp, [BK(z)], [e_], scale=-1.0)
                        ACT(spm[m].ap[:, c0:512], e_.ap[:, c0:512], AF.Ln, [e_], [spm[m]], bias=1.0)

                def st2b(it, n, h2):
                    c4, i, first, last = it
                    c0, r = cz(it)
                    z = ZB[n % 2][h2]
                    m = (n % 2) * 2 + h2
                    TT(spp[m].ap[:, c0:512], ps[:, z, c0:512], spm[m].ap[:, c0:512], ALU.add, [BK(z), spm[m]], [spp[m]])
                    if r >= 0:
                        TT(spp[m].ap[:, c0:c0 + 128], spp[m].ap[:, c0:c0 + 128], TRI, ALU.mult, [spp[m], cT["Mr"]], [spp[m]])
                    MM(ps[:, TBK[h2], c0:512], Um[:], spp[m].ap[:, c0:512], first, last, [cT["U"], spp[m]], [BK(TBK[h2])])

                def st3(it, n, h2):
                    c4, i, first, last = it
                    c0, r = cz(it)
                    m = (n % 2) * 2 + h2
                    TT(wb[m].ap[:, c0:512], ps[:, TBK[h2], c0:512], spm[m].ap[:, c0:512], ALU.add, [BK(TBK[h2]), spm[m]], [wb[m]])
                    if not last:
                        MM(ps[:, TBK[h2], c0:512], Lm[:], spp[m].ap[:, c0:512], False, False, [cT["L"], spp[m]], [BK(TBK[h2])])
                    a_ = ab[m]
                    ACT(a_.ap[:, c0:512], wb[m].ap[:, c0:512], AF.Exp, [wb[m]], [a_], scale=-1.0)
                    if r >= 0:
                        TT(a_.ap[:, c0:c0 + 128], a_.ap[:, c0:c0 + 128], TRI, ALU.mult, [a_, cT["Mr"]], [a_])

                def st3av(it, n):
                    c4, i, first, last = it
                    c0, r = cz(it)
                    ob = 5 + c4 % 2
                    for h2 in range(2):
                        a_ = ab[(n % 2) * 2 + h2]
                        MM(ps[RR[h2], ob, c0:512], d["v"].ap[:, i, h2 * 64:h2 * 64 + 64], a_.ap[:, c0:512], first, last,
                           [d["v"].t(i * 128, i * 128 + 128), a_], [BK(ob, None, h2)])
                    if last:
                        ACT(d["o"].ap[:, c4 * 512:(c4 + 1) * 512], ps[:, ob, :], AF.Copy, [BK(ob)],
                            [d["o"].t(c4 * 512, c4 * 512 + 512)])

                n_ = len(items)
                for j in range(n_ + 2):
                    if j < n_:
                        st1(items[j], j)
                    if 0 <= j - 1 < n_:
                        st2a(items[j - 1], j - 1)
                    for h2 in range(2):
                        if 0 <= j - 2 < n_:
                            st3(items[j - 2], j - 2, h2)
                        if 0 <= j - 1 < n_:
                            st2b(items[j - 1], j - 1, h2)
                    if 0 <= j - 2 < n_:
                        st3av(items[j - 2], j - 2)

            load_weights(0)
            for p in range(8):
                if stop_after == "lw":
                    return
                if p + 1 < 8:
                    load_weights(p + 1)
                project(p)
                if stop_after in ("proj", "pj1", "pj2", "pj3", "pj4"):
                    return
                if moba:
                    moba_select(p)
                    if stop_after == "sel":
                        return
                    moba_core(p)
                else:
                    sb_core(p)
                if stop_after == "core":
                    return
                if p % 2 == 1:
                    out_proj2(p - 1)
                if stop_after == "op0":
                    return

        def ffn(l):
            arena.off = 0
            gT = [arena.alloc(4 * S, BF16, shape=[4, S]) for _ in range(2)]
            wu = [arena.alloc(8 * 1024, BF16, shape=[8, 1024]) for _ in range(2)]
            wd = [arena.alloc(4 * 1024, BF16, shape=[4, 1024]) for _ in range(2)]
            tg = [[arena.alloc(512, F32) for _ in range(2)] for _ in range(2)]
            tv = [[arena.alloc(512, F32) for _ in range(2)] for _ in range(2)]
            sg = [arena.alloc(512, F32) for _ in range(2)]
            chunks = [(0, 410), (410, 410), (820, 410), (1230, 410), (1640, 408)]

            def loadw(G):
                nt = 4 if G < 5 else 2
                DMA("pool", wu[G % 2].ap, wup_d[l, G].rearrange("r (c n) -> r c n", c=8), [], [wu[G % 2]], max_dma_last_dim=4096)
                DMA("pool", wd[G % 2].ap[:, 0:nt, :], wdn_d[l, G * 512:G * 512 + nt * 128, :].rearrange("(j r) d -> r j d", r=128),
                    [], [wd[G % 2]], max_dma_last_dim=4096)

            loadw(0)
            kk = 0
            dn = 0
            for G in range(6):
                nt = 4 if G < 5 else 2
                if G + 1 < 6:
                    loadw(G + 1)
                g_ = gT[G % 2]
                wu_ = wu[G % 2]
                wd_ = wd[G % 2]
                for jj in range(nt):
                    j = 4 * G + jj
                    jv = 22 + j
                    for (s0, n) in chunks:
                        st = kk % 2
                        kk += 1
                        gb, vb = 2 * st, 2 * st + 1
                        for c in range(8):
                            MM(ps[:, gb, 0:n + 2], wu_.ap[:, c, jj * 128:(jj + 1) * 128], yrhs(c, s0, n + 2), c == 0, c == 7,
                               [wu_, yts(c, s0, s0 + n + 2), cT["ypad"]], [bankT[gb]])
                        for c in range(8):
                            MM(ps[:, vb, 0:n + 2], wu_.ap[:, c, 512 + jj * 128:512 + (jj + 1) * 128], yrhs(c, s0, n + 2), c == 0, c == 7,
                               [wu_, yts(c, s0, s0 + n + 2), cT["ypad"]], [bankT[vb]])
                        a0, a1 = tg[st]
                        v0, v1 = tv[st]
                        cw = lambda k_, jx: convp[:, l, k_, jx:jx + 1]
                        ACT(a0.ap[:, 0:n], ps[:, gb, 2:n + 2], AF.Identity, [bankT[gb], cT["convp"]], [a0], scale=cw(2, j), bias=cw(3, j))
                        ACT(v0.ap[:, 0:n], ps[:, vb, 2:n + 2], AF.Identity, [bankT[vb], cT["convp"]], [v0], scale=cw(2, jv), bias=cw(3, jv))
                        STT(a1.ap[:, 0:n], ps[:, gb, 1:n + 1], cw(1, j), a0.ap[:, 0:n], ALU.mult, ALU.add, [bankT[gb], a0, cT["convp"]], [a1])
                        STT(a0.ap[:, 0:n], ps[:, gb, 0:n], cw(0, j), a1.ap[:, 0:n], ALU.mult, ALU.add, [bankT[gb], a1, cT["convp"]], [a0])
                        STT(v1.ap[:, 0:n], ps[:, vb, 1:n + 1], cw(1, jv), v0.ap[:, 0:n], ALU.mult, ALU.add, [bankT[vb], v0, cT["convp"]], [v1])
                        STT(v0.ap[:, 0:n], ps[:, vb, 0:n], cw(0, jv), v1.ap[:, 0:n], ALU.mult, ALU.add, [bankT[vb], v1, cT["convp"]], [v0])
                        ACT(sg[st].ap[:, 0:n], a0.ap[:, 0:n], AF.Silu, [a0], [sg[st]])
                        PTT(g_.ap[:, jj, s0:s0 + n], sg[st].ap[:, 0:n], v0.ap[:, 0:n], ALU.mult, [sg[st], v0],
                           [g_.t(jj * S + s0, jj * S + s0 + n)])
                for tc in range(4):
                    for dt_ in range(8):
                        sl = slice(tc * 512, (tc + 1) * 512)
                        b = 4 + dn % 2
                        dn += 1
                        for jj in range(nt):
                            MM(ps[:, b, :], wd_.ap[:, jj, dt_ * 128:(dt_ + 1) * 128], g_.ap[:, jj, sl], jj == 0, jj == nt - 1,
                               [wd_, g_.t(jj * S + tc * 512, jj * S + tc * 512 + 512)], [bankT[b]])
                        TT(hT[:, dt_, sl], ps[:, b, :], hT[:, dt_, sl], ALU.add, [bankT[b], hT_T[dt_][tc]], [hT_T[dt_][tc]])

        done = False
        for l in range(nlayers):
            if stop_after == "const":
                dump_h()
                done = True
                break
            norm(2 * l)
            if stop_after == "norm":
                dump_h()
                done = True
                break
            attention(l)
            if stop_after in ("attn%d" % l, "lw", "proj", "sel", "core", "op0", "pj1", "pj2", "pj3", "pj4"):
                dump_h()
                done = True
                break
            norm(2 * l + 1)
            ffn(l)
            if stop_after == "ffn%d" % l:
                dump_h()
                done = True
                break
        if not done:
            norm(4, final=True)

        P.finalize()
        sems = {e: es.enter_context(nc.semaphore("s_" + e)) for e in COMPUTE}
        dsems = {(q, k): es.enter_context(nc.semaphore("d_%s%d" % (q, k))) for q in QUEUES for k in range(NDMASEM)}
        block = es.enter_context(nc.Block())
        P.emit(block, sems, dsems, out_final)
    return nc, len(P.ops)


def _rel_bucket_np(dist):
    n = np.maximum(dist, 0)
    max_exact = 16
    nf = np.maximum(n, 1).astype(np.float32)
    large = max_exact + (np.log(nf / np.float32(max_exact)) / np.float32(np.log(128 / 16)) * np.float32(16)).astype(np.int32)
    large = np.minimum(large, 31)
    return np.where(n < max_exact, n, large)


def _constants():
    bf = ml_dtypes.bfloat16
    j = np.arange(128)[:, None]
    s = np.arange(128)[None, :]
    U = (j > s).astype(bf)
    L = (j <= s).astype(bf)
    xx = np.arange(896)[None, :]
    Mr = (j < xx - 384).astype(np.float32)
    gmask = np.zeros((128, 16, 8), np.float32)
    for s_ in range(16):
        qb = 4 + (s_ % 8) // 2
        gmask[:, s_, qb:] = -1e30
    koh = np.zeros((8, S), np.float32)
    for n in range(8):
        koh[n, n * 256:(n + 1) * 256] = 1.0
    return dict(idb=np.eye(128).astype(bf), Umat=U, Lmat=L,
                Mr=Mr.astype(bf), koh=koh.astype(bf), gmask=gmask.reshape(128, 128))


def _layout_inputs(x, attn_norm, w_qkv, w_o, rel_bias, ffn_norm, w_up, conv_w, conv_b, w_down, final_norm):
    f = np.float32
    x = np.asarray(x, f)
    vecs = [attn_norm[0], ffn_norm[0], attn_norm[1], ffn_norm[1], final_norm]
    gains = np.stack([np.asarray(v, f).reshape(8, 128).T for v in vecs], axis=1).reshape(128, 40)
    w_qkv = np.asarray(w_qkv, f)
    wq = w_qkv.reshape(2, 8, 128, 3, 8, 128)
    wqkv = np.ascontiguousarray(wq.transpose(0, 4, 2, 1, 3, 5)).reshape(2, 8, 128, 8 * 384)
    w_up = np.asarray(w_up, f)
    wup = np.zeros((2, 6, 128, 8, 1024), f)
    for G in range(6):
        nt = 4 if G < 5 else 2
        g = w_up[:, :, G * 512:G * 512 + nt * 128].reshape(2, 8, 128, nt * 128).transpose(0, 2, 1, 3)
        v = w_up[:, :, DFF + G * 512:DFF + G * 512 + nt * 128].reshape(2, 8, 128, nt * 128).transpose(0, 2, 1, 3)
        wup[:, G, :, :, 0:nt * 128] = g
        wup[:, G, :, :, 512:512 + nt * 128] = v
    wup = wup.reshape(2, 6, 128, 8 * 1024)
    cw = np.asarray(conv_w, f)
    cbv = np.asarray(conv_b, f)
    cp = np.concatenate([cw, cbv[:, None, :]], axis=1)
    convp = np.ascontiguousarray(cp.reshape(2, 4, 44, 128).transpose(3, 0, 1, 2)).reshape(128, 2 * 4 * 44)
    common = dict(gains=np.ascontiguousarray(gains), wqkv=wqkv, wo=np.ascontiguousarray(np.asarray(w_o, f)), wup=wup,
                  wdn=np.ascontiguousarray(np.asarray(w_down, f)), convp=convp,
                  relT=np.ascontiguousarray(np.asarray(rel_bias, f).T))
    common.update(_constants())
    kl = np.arange(128)[:, None]
    xx = np.arange(256)[None, :]
    dd = xx - kl
    gidx = np.where(dd >= 0, _rel_bucket_np(dd), 32)
    rb_ext = np.concatenate([np.asarray(rel_bias, f), np.full((16, 1), NEGV, f)], axis=1)
    common["gtab"] = np.ascontiguousarray(rb_ext[:, gidx])
    maps = []
    for b in range(8):
        m = dict(common)
        m["xT"] = np.ascontiguousarray(x[b].T)
        maps.append(m)
    return maps


_NC_CACHE = {}


def kernel(x, attn_norm, w_qkv, w_o, rel_bias, ffn_norm, w_up, conv_w, conv_b, w_down, final_norm, _stop_after=None, _nlayers=2, _ncores=8):
    key = (_stop_after, _nlayers)
    if key not in _NC_CACHE:
        _NC_CACHE[key] = build(_nlayers, _stop_after)[0]
    nc = _NC_CACHE[key]
    maps = _layout_inputs(x, attn_norm, w_qkv, w_o, rel_bias, ffn_norm, w_up, conv_w, conv_b, w_down, final_norm)
    if _ncores < 8:
        res = run_bass_kernel_spmd(nc, maps[:_ncores], core_ids=list(range(_ncores)))
        return np.stack([np.asarray(r["outT"]).T for r in res.results], axis=0)
    res = run_bass_kernel_spmd(nc, maps, core_ids=list(range(8)))
    out = np.stack([np.asarray(r["outT"]).T for r in res.results], axis=0)
    return np.ascontiguousarray(out.astype(np.float32))
```

```python
import contextlib
import numpy as np
import ml_dtypes
import concourse.bass as bass
import concourse.mybir as mybir
from concourse.bass_utils import run_bass_kernel_spmd

F32 = mybir.dt.float32
BF16 = mybir.dt.bfloat16
F32R = mybir.dt.float32r
AF = mybir.ActivationFunctionType
ALU = mybir.AluOpType
AX = mybir.AxisListType

S = 2048
D = 1024
DFF = 2816
NEGV = -30000.0
COMPUTE = ("pe", "act", "dve", "pool")
QUEUES = ("sp", "pool")
NDMASEM = 12


class T:
    __slots__ = ("w", "rs")

    def __init__(self):
        self.w = None
        self.rs = []


class TP(T):
    __slots__ = ()


def _flat(xs):
    out = []
    for x in xs:
        if isinstance(x, T):
            out.append(x)
        elif isinstance(x, (list, tuple)):
            out.extend(_flat(x))
        else:
            out.extend(x.ts)
    return out


class Prog:
    def __init__(self):
        self.ops = []
        self.dma_rr = {q: 0 for q in QUEUES}
        self.dma_tot = {}

    def op(self, eng, fn, r=(), w=(), dma=False):
        i = len(self.ops)
        r = _flat(r)
        w = _flat(w)
        w = w + [b for b in r if isinstance(b, TP)]
        r = [b for b in r if not isinstance(b, TP)]
        deps = {}
        for b in r:
            if b.w is not None:
                deps[b.w] = "raw"
        for b in w:
            if b.w is not None:
                deps.setdefault(b.w, "waw")
            for x in b.rs:
                deps.setdefault(x, "war")
        for b in r:
            b.rs.append(i)
        for b in w:
            b.w = i
            b.rs = []
        deps.pop(i, None)
        o = dict(eng=eng, fn=fn, deps=deps, dma=dma, sig=False, cnt=None, semi=None, semv=None, prev=None)
        if dma:
            k = self.dma_rr[eng]
            self.dma_rr[eng] = (k + 1) % NDMASEM
            key = (eng, k)
            o["prev"] = self.dma_tot.get(key, 0)
            self.dma_tot[key] = o["prev"] + 16
            o["semi"] = key
            o["semv"] = o["prev"] + 16
        self.ops.append(o)
        return i

    def finalize(self):
        ops = self.ops
        for o in ops:
            need = {}
            for d, kind in o["deps"].items():
                p = ops[d]
                if p["dma"]:
                    need[d] = kind
                    continue
                if p["eng"] == o["eng"] and not o["dma"]:
                    if o["eng"] == "pe" or kind != "raw":
                        continue
                need[d] = kind
                p["sig"] = True
            o["need"] = need
        cnt = {e: 0 for e in set(COMPUTE + QUEUES)}
        for o in ops:
            if o["sig"] and not o["dma"]:
                cnt[o["eng"]] += 1
                o["cnt"] = cnt[o["eng"]]

    def emit(self, block, sems, dsems, out_final):
        ops = self.ops
        engs = {"pe": block.tensor, "act": block.scalar, "dve": block.vector, "pool": block.gpsimd, "sp": block.sync}
        for ename, deco in engs.items():
            mine = [o for o in ops if o["eng"] == ename]

            def body(e, mine=mine, ename=ename):
                waited = {}
                for o in mine:
                    for d in o["need"]:
                        p = ops[d]
                        if p["dma"]:
                            key, v, s = ("d",) + p["semi"], p["semv"], dsems[p["semi"]]
                        else:
                            key, v, s = p["eng"], p["cnt"], sems[p["eng"]]
                        if waited.get(key, 0) >= v:
                            continue
                        waited[key] = v
                        e.wait_ge(s, v)
                    if o["dma"]:
                        key = ("d",) + o["semi"]
                        if o["prev"] > 0 and waited.get(key, 0) < o["prev"]:
                            e.wait_ge(dsems[o["semi"]], o["prev"])
                            waited[key] = o["prev"]
                    ins = o["fn"](e)
                    if o["dma"]:
                        ins.then_inc(dsems[o["semi"]], 16)
                    elif o["sig"]:
                        ins.then_inc(sems[o["eng"]], 1)
                if ename == "sp":
                    for d in out_final:
                        p = ops[d]
                        e.wait_ge(dsems[p["semi"]], p["semv"])

            deco(body)


class Buf:
    def __init__(self, ap, ts, off, slot, esz):
        self.ap = ap
        self.ts = ts
        self.off = off
        self.slot = slot
        self.esz = esz

    def t(self, lo, hi):
        a = (self.off + lo * self.esz) // self.slot
        b = (self.off + hi * self.esz - 1) // self.slot
        base = self.off // self.slot
        return self.ts[a - base:b - base + 1]


class Arena:
    SLOT = 512

    def __init__(self, ap_bf16, nbytes):
        self.ap = ap_bf16
        self.n = nbytes
        self.T = [T() for _ in range(nbytes // self.SLOT)]
        self.off = 0

    def alloc(self, nelem, dt=BF16, parts=128, shape=None):
        esz = 4 if dt in (F32, F32R) else 2
        nb = nelem * esz
        nb = (nb + self.SLOT - 1) // self.SLOT * self.SLOT
        off = self.off
        self.off += nb
        assert self.off <= self.n, ("arena overflow", self.off, self.n)
        v = self.ap[0:parts, off // 2:(off + nelem * esz) // 2]
        if dt != BF16:
            v = v.bitcast(dt)
        if shape is not None:
            names = " ".join("abcdef"[i] for i in range(len(shape)))
            v = v.rearrange("p (%s) -> p %s" % (names, names), **{"abcdef"[i]: shape[i] for i in range(len(shape))})
        ts = self.T[off // self.SLOT:(off + nb) // self.SLOT]
        return Buf(v, ts, off, self.SLOT, esz)


def build(nlayers=2, stop_after=None):
    nc = bass.Bass("TRN2", target_bir_lowering=False)

    def DI(name, shape, dt=F32):
        return nc.dram_tensor(name, shape, dt, kind="ExternalInput").ap()

    xT_d = DI("xT", [D, S])
    gains_d = DI("gains", [128, 40])
    wqkv_d = DI("wqkv", [2, 8, 128, 8 * 384])
    wo_d = DI("wo", [2, D, D])
    wup_d = DI("wup", [2, 6, 128, 8 * 1024])
    wdn_d = DI("wdn", [2, DFF, D])
    convp_d = DI("convp", [128, 2 * 4 * 44])
    relT_d = DI("relT", [32, 16])
    idb_d = DI("idb", [128, 128], BF16)
    U_d = DI("Umat", [128, 128], BF16)
    L_d = DI("Lmat", [128, 128], BF16)
    Mr_d = DI("Mr", [128, 896], BF16)
    koh_d = DI("koh", [8, S], BF16)
    gmask_d = DI("gmask", [128, 128])
    outT_d = nc.dram_tensor("outT", [D, S], F32, kind="ExternalOutput").ap()
    gscr_d = DI("gtab", [16, 128, 256])

    P = Prog()
    ARENA_BYTES = 100 * 1024
    es = contextlib.ExitStack()
    with es:
        hT = es.enter_context(nc.sbuf_tensor("hT", [128, 8, S], F32))
        yT = es.enter_context(nc.sbuf_tensor("yT", [128, 8, S + 4], BF16))
        arena_t = es.enter_context(nc.sbuf_tensor("arena", [128, ARENA_BYTES // 2], BF16))
        gains = es.enter_context(nc.sbuf_tensor("gains_sb", [128, 40], F32))
        convp = es.enter_context(nc.sbuf_tensor("convp_sb", [128, 2, 4, 44], F32))
        idb = es.enter_context(nc.sbuf_tensor("idb_sb", [128, 128], BF16))
        Um = es.enter_context(nc.sbuf_tensor("U_sb", [128, 128], BF16))
        Lm = es.enter_context(nc.sbuf_tensor("L_sb", [128, 128], BF16))
        Mx = es.enter_context(nc.sbuf_tensor("Mr_sb", [128, 896], BF16))
        onesf = es.enter_context(nc.sbuf_tensor("onesf", [128, 128], F32))
        onesr = es.enter_context(nc.sbuf_tensor("onesr", [128, 128], F32R))
        onesb = es.enter_context(nc.sbuf_tensor("onesb", [128, 64], BF16))
        sqr = es.enter_context(nc.sbuf_tensor("sqr", [128, 2, 512], F32R))
        sqT = [T(), T()]
        cb = es.enter_context(nc.sbuf_tensor("cb", [128, 16], F32))
        kms = es.enter_context(nc.sbuf_tensor("kms", [128, 8], F32))
        kmb = es.enter_context(nc.sbuf_tensor("kmb", [128, 8], BF16))
        ps = es.enter_context(nc.psum_tensor("ps", [128, 8, 512], F32))
        psb = ps[:, 7, :].bitcast(BF16)

        arena = Arena(arena_t[:], ARENA_BYTES)
        hT_T = [[T() for _ in range(4)] for _ in range(8)]
        yT_T = [[T() for _ in range(4)] for _ in range(8)]
        bankQ = [[TP(), TP()] for _ in range(8)]

        def BK(b, half=None, rows=None):
            rs = [0, 1] if rows is None else [rows]
            return [bankQ[b][r] for r in rs]

        class _BT:
            def __getitem__(self, b):
                return BK(b)
        bankT = _BT()
        cT = {k: T() for k in ("gains", "convp", "idb", "U", "L", "Mr", "ones", "cb", "relT", "onehot", "negm", "Fsb", "fscr",
                               "g8", "top8", "npad0", "npad1", "kms", "kmb", "ypad")}

        def MM(out, lhsT, rhs, start, stop, r, w):
            P.op("pe", lambda e: e.matmul(out, lhsT=lhsT, rhs=rhs, start=start, stop=stop), r=r, w=w)

        def ACT(out, in_, func, r, w, **kw):
            P.op("act", lambda e: e.activation(out=out, in_=in_, func=func, **kw), r=r, w=w)

        def TT(out, in0, in1, op, r, w):
            P.op("dve", lambda e: e.tensor_tensor(out=out, in0=in0, in1=in1, op=op), r=r, w=w)

        def TS(out, in0, s1, s2, op0, op1, r, w):
            P.op("dve", lambda e: e.tensor_scalar(out=out, in0=in0, scalar1=s1, scalar2=s2, op0=op0, op1=op1), r=r, w=w)

        def STT(out, in0, scalar, in1, op0, op1, r, w):
            P.op("dve", lambda e: e.scalar_tensor_tensor(out=out, in0=in0, scalar=scalar, in1=in1, op0=op0, op1=op1), r=r, w=w)

        def PTT(out, in0, in1, op, r, w):
            P.op("pool", lambda e: e.tensor_tensor(out=out, in0=in0, in1=in1, op=op), r=r, w=w)

        def VCOPY(out, in_, r, w):
            P.op("dve", lambda e: e.tensor_copy(out=out, in_=in_), r=r, w=w)

        def VOP(fn, r, w):
            P.op("dve", fn, r=r, w=w)

        def DMA(q, out, in_, r, w, **kw):
            return P.op(q, lambda e: e.dma_start(out=out, in_=in_, **kw), r=r, w=w, dma=True)

        DMA("sp", gains[:], gains_d[:], [], [cT["gains"]])
        DMA("sp", convp[:].rearrange("p a b c -> p (a b c)"), convp_d[:], [], [cT["convp"]])
        DMA("sp", idb[:], idb_d[:], [], [cT["idb"]])
        DMA("sp", Um[:], U_d[:], [], [cT["U"]])
        DMA("sp", Lm[:], L_d[:], [], [cT["L"]])
        DMA("sp", Mx[:], Mr_d[:], [], [cT["Mr"]])
        DMA("sp", cb[:], relT_d[31:32, :].partition_broadcast(128), [], [cT["cb"]])
        for tc in range(4):
            for c in range(8):
                DMA("sp", hT[:, c, tc * 512:(tc + 1) * 512], xT_d[c * 128:(c + 1) * 128, tc * 512:(tc + 1) * 512], [], [hT_T[c][tc]])
        VOP(lambda e: e.memset(onesf[:], 1.0), [], [cT["ones"]])
        VCOPY(onesr[:], onesf[:], [cT["ones"]], [cT["ones"]])
        VCOPY(onesb[:], onesf[:, 0:64], [cT["ones"]], [cT["ones"]])
        VOP(lambda e: e.memset(yT[:, :, 0:2], 0.0), [], [cT["ypad"]])
        out_final = []

        def norm(gi, final=False):
            arena.off = 0
            lnt = [arena.alloc(512, F32) for _ in range(2)]
            stg = [arena.alloc(512, F32) for _ in range(3)] if final else None
            k = 0
            for tc in range(4):
                sl = slice(tc * 512, (tc + 1) * 512)
                b = 5 + (tc % 2)
                for c in range(8):
                    sq = sqT[k % 2]
                    sqa = sqr[:, k % 2, :]
                    k += 1
                    ACT(sqa, hT[:, c, sl], AF.Square, [hT_T[c][tc]], [sq])
                    MM(ps[:, b, :], onesr[:], sqa, c == 0, c == 7, [sq, cT["ones"]], [bankT[b]])
                lt = lnt[tc % 2]
                ACT(lt.ap, ps[:, b, :], AF.Ln, [bankT[b]], [lt], scale=1.0 / D, bias=1e-6)
                ACT(ps[:, b, :], lt.ap, AF.Exp, [lt], [bankT[b]], scale=-0.5)
                for c in range(8):
                    if not final:
                        STT(yT[:, c, 2 + tc * 512:2 + (tc + 1) * 512], hT[:, c, sl], gains[:, gi * 8 + c:gi * 8 + c + 1], ps[:, b, :],
                            ALU.mult, ALU.mult, [hT_T[c][tc], cT["gains"], bankT[b], cT["ypad"]], [yT_T[c][tc]])
                    else:
                        st = stg[(tc * 8 + c) % 3]
                        STT(st.ap, hT[:, c, sl], gains[:, gi * 8 + c:gi * 8 + c + 1], ps[:, b, :],
                            ALU.mult, ALU.mult, [hT_T[c][tc], cT["gains"], bankT[b]], [st])
                        out_final.append(DMA("sp", outT_d[c * 128:(c + 1) * 128, sl], st.ap, [st], []))

        def dump_h():
            for c in range(8):
                out_final.append(DMA("sp", outT_d[c * 128:(c + 1) * 128, :], hT[:, c, :], [hT_T[c]], []))

        def yrhs(c, lo, n):
            return yT[:, c, lo:lo + n]

        def yts(c, lo, hi):
            a = max(0, (lo - 2)) // 512
            b = min(3, max(0, hi - 3) // 512)
            return [yT_T[c][i] for i in range(a, b + 1)]

        def attention(l):
            moba = (l % 2 == 0)
            arena.off = 0
            sets = []
            for s_ in range(2):
                d = {}
                d["q"] = arena.alloc(S, BF16)
                d["k"] = arena.alloc(S, BF16)
                if moba:
                    d["qx"] = arena.alloc(S, BF16)
                    d["kx"] = arena.alloc(S, BF16)
                    d["Tb"] = arena.alloc(4 * 128, F32, shape=[2, 2, 128])
                d["v"] = arena.alloc(16 * 128, BF16, shape=[16, 128])
                d["o"] = arena.alloc(S, BF16)
                d["wq"] = arena.alloc(8 * 384, BF16, shape=[8, 384])
                sets.append(d)
            wos = [arena.alloc(D, BF16) for _ in range(4)]
            if moba:
                Pt = [arena.alloc(512, BF16) for _ in range(6)]
                g16b = arena.alloc(128, F32, shape=[16, 8])
                top16b = arena.alloc(128, F32, shape=[16, 8])
                npad16b = arena.alloc(16 * 72, BF16, shape=[16, 72])
                gmaskb = arena.alloc(128, F32)
                g16, top16, npad16, gmask = g16b.ap, top16b.ap, npad16b.ap, gmaskb.ap
                VOP(lambda e: e.memset(npad16, 0.0), [], [npad16b])
                DMA("sp", gmask, gmask_d[:], [], [gmaskb])
                tmpb = [arena.alloc(128, F32) for _ in range(4)]
                rden = [arena.alloc(512, F32) for _ in range(2)]
            else:
                eb = [arena.alloc(512, F32) for _ in range(4)]
                spm = [arena.alloc(512, F32) for _ in range(4)]
                spp = [arena.alloc(512, BF16) for _ in range(4)]
                wb = [arena.alloc(512, F32) for _ in range(4)]
                ab = [arena.alloc(512, BF16) for _ in range(4)]
                tmpd = [arena.alloc(512, F32) for _ in range(4)]

            def load_weights(p):
                d = sets[p % 2]
                DMA("pool", d["wq"].ap, wqkv_d[l, p].rearrange("r (c n) -> r c n", c=8), [], [d["wq"]], max_dma_last_dim=4096)
                DMA("pool", wos[p % 4].ap, wo_d[l, p * 128:(p + 1) * 128, :], [], [wos[p % 4]], max_dma_last_dim=4096)
                if moba:
                    for h2 in range(2):
                        DMA("sp", d["kx"].ap[h2 * 64:h2 * 64 + 8, :], koh_d[:, :], [], [d["kx"]])
                        h = 2 * p + h2
                        DMA("sp", d["Tb"].ap[:, h2, :, :], gscr_d[h].rearrange("k (w q) -> k w q", w=2), [], [d["Tb"]])

            pj_rot = [6, 0, 1] if moba else [0, 7, 3]
            pjc = [0]

            def pjbank():
                b = pj_rot[pjc[0] % 3]
                pjc[0] += 1
                return b

            def project(p):
                d = sets[p % 2]
                wq = d["wq"]
                for tc in range(4):
                    sl = slice(tc * 512, (tc + 1) * 512)
                    b = pjbank()
                    for c in range(8):
                        MM(ps[:, b, :], wq.ap[:, c, 0:128], yrhs(c, 2 + tc * 512, 512), c == 0, c == 7, [wq, yT_T[c][tc]], [bankT[b]])
                    ACT(d["q"].ap[:, sl], ps[:, b, :], AF.Copy, [bankT[b]], [d["q"].t(tc * 512, tc * 512 + 512)], scale=0.125)
                    b = pjbank()
                    for c in range(8):
                        MM(ps[:, b, :], wq.ap[:, c, 128:256], yrhs(c, 2 + tc * 512, 512), c == 0, c == 7, [wq, yT_T[c][tc]], [bankT[b]])
                    if moba:
                        for hf in range(2):
                            cs = slice(tc * 512 + hf * 256, tc * 512 + hf * 256 + 256)
                            ACT(d["k"].ap[:, cs], ps[:, b, hf * 256:hf * 256 + 256], AF.Copy, [bankT[b]],
                                [d["k"].t(cs.start, cs.stop), cT["kms"]], accum_out=kms[:, tc * 2 + hf:tc * 2 + hf + 1])
                    else:
                        VCOPY(d["k"].ap[:, sl], ps[:, b, :], [bankT[b]], [d["k"].t(tc * 512, tc * 512 + 512)])
                if moba:
                    ACT(kmb[:, :], kms[:, :], AF.Copy, [cT["kms"]], [cT["kmb"]], scale=1.0 / 256)
                for tt in range(16):
                    b = pjbank()
                    for c in range(8):
                        MM(ps[:, b, 0:128], yrhs(c, 2 + tt * 128, 128), wq.ap[:, c, 256:384], c == 0, c == 7, [wq, yT_T[c][tt // 4]], [bankT[b]])
                    if tt % 2 == 0:
                        VCOPY(d["v"].ap[:, tt, :], ps[:, b, 0:128], [bankT[b]], [d["v"].t(tt * 128, tt * 128 + 128)])
                    else:
                        ACT(d["v"].ap[:, tt, :], ps[:, b, 0:128], AF.Copy, [bankT[b]], [d["v"].t(tt * 128, tt * 128 + 128)])

            def out_proj2(p0):
                for tc in range(4):
                    for dt_ in range(8):
                        sl = slice(tc * 512, (tc + 1) * 512)
                        b = pjbank()
                        for k_, p_ in enumerate((p0, p0 + 1)):
                            d = sets[p_ % 2]
                            MM(ps[:, b, :], wos[p_ % 4].ap[:, dt_ * 128:(dt_ + 1) * 128], d["o"].ap[:, sl], k_ == 0, k_ == 1,
                               [wos[p_ % 4], d["o"].t(tc * 512, tc * 512 + 512)], [bankT[b]])
                        TT(hT[:, dt_, sl], ps[:, b, :], hT[:, dt_, sl], ALU.add, [bankT[b], hT_T[dt_][tc]], [hT_T[dt_][tc]])


            def moba_select(p):
                d = sets[p % 2]
                for h2 in range(2):
                    R = slice(h2 * 64, h2 * 64 + 64)
                    for j in range(8):
                        qt = 8 + j
                        c = (h2 * 8 + j) * 8
                        MM(ps[:, 6, c:c + 8], d["q"].ap[R, qt * 128:(qt + 1) * 128], kmb[R, :], True, True,
                           [d["q"].t(qt * 128, qt * 128 + 128), cT["kmb"]], [bankT[6]])
                TT(g16.rearrange("p a b -> p (a b)"), ps[:, 6, 0:128], gmask, ALU.add, [bankT[6], gmaskb], [g16b])
                for s_ in range(16):
                    VOP(lambda e, s_=s_: e.max(out=top16[:, s_, :], in_=g16[:, s_, :]), [g16b], [top16b])
                for s_ in range(16):
                    h2 = s_ // 8
                    TS(npad16[:, s_, h2 * 64:h2 * 64 + 8], g16[:, s_, :], top16[:, s_, 2:3], NEGV, ALU.is_lt, ALU.mult,
                       [g16b, top16b], [npad16b])
                for h2 in range(2):
                    for j in range(8):
                        P.op("pe", lambda e, h2=h2, j=j: e.transpose(out=psb[0:72, j * 128:(j + 1) * 128], in_=npad16[:, h2 * 8 + j, :], identity=idb[:]),
                             r=[npad16b, cT["idb"]], w=[BK(7)])
                    ACT(d["qx"].ap[h2 * 64:h2 * 64 + 8, 1024:2048], psb[h2 * 64:h2 * 64 + 8, 0:1024], AF.Copy, [BK(7)],
                        [d["qx"].t(1024, 2048)])

            def moba_core(p):
                d = sets[p % 2]
                items = []
                for C in range(4):
                    base = 512 * C
                    its = []
                    for kt in range(4 * C):
                        its.append((kt, base, 512))
                    its.append((4 * C, base, 512))
                    its.append((4 * C + 1, base + 128, 384))
                    its.append((4 * C + 2, base + 256, 256))
                    its.append((4 * C + 3, base + 384, 128))
                    for ii, (kt, q0, W) in enumerate(its):
                        items.append((C, kt, q0, W, ii == 0, ii == len(its) - 1))
                sc = [0]
                SBK = ((0, 1), (6, 7))

                def stageA(it, idx):
                    C, kt, q0, W, first, last = it
                    n = kt // 2
                    c0 = q0 - 512 * C
                    a0 = max(q0, 256 * (n + 1), 1024)
                    for h2 in range(2):
                        R = slice(h2 * 64, h2 * 64 + 64)
                        sb_ = SBK[idx % 2][h2]
                        MM(ps[:, sb_, c0:c0 + W], d["k"].ap[R, kt * 128:(kt + 1) * 128], d["q"].ap[R, q0:q0 + W], True, True,
                           [d["k"].t(kt * 128, kt * 128 + 128), d["q"].t(q0, q0 + W)], [BK(sb_)])
                    if a0 < q0 + W:
                        ca = a0 - 512 * C
                        for h2 in range(2):
                            X = slice(h2 * 64, h2 * 64 + 8)
                            sb_ = SBK[idx % 2][h2]
                            MM(ps[:, sb_, ca:c0 + W], d["kx"].ap[X, kt * 128:(kt + 1) * 128], d["qx"].ap[X, a0:q0 + W], False, True,
                               [d["kx"].t(kt * 128, kt * 128 + 128), d["qx"].t(a0, q0 + W)], [BK(sb_)])

                def stageB(it, idx):
                    C, kt, q0, W, first, last = it
                    c0 = q0 - 512 * C
                    kinds = []
                    for u in range(W // 128):
                        delta = q0 + u * 128 - kt * 128
                        kinds.append(0 if delta == 0 else (1 if delta == 128 else 2))
                    groups = []
                    u = 0
                    while u < len(kinds):
                        if kinds[u] == 2:
                            v_ = u
                            while v_ < len(kinds) and kinds[v_] == 2:
                                v_ += 1
                            groups.append((2, u, v_))
                            u = v_
                        else:
                            groups.append((kinds[u], u, u + 1))
                            u += 1
                    for h2 in range(2):
                        h = 2 * p + h2
                        sb_ = SBK[idx % 2][h2]
                        pt = Pt[(idx % 3) * 2 + h2]
                        for kd, u0, u1 in groups:
                            cs = slice(c0 + u0 * 128, c0 + u1 * 128)
                            if kd == 2:
                                ACT(pt.ap[:, cs], ps[:, sb_, cs], AF.Exp, [BK(sb_), cT["cb"]], [pt], bias=cb[:, h:h + 1], scale=1.0)
                            else:
                                tb = tmpb[sc[0] % 4]
                                sc[0] += 1
                                TT(tb.ap, ps[:, sb_, cs], d["Tb"].ap[:, h2, kd, :], ALU.add, [BK(sb_), d["Tb"]], [tb])
                                ACT(pt.ap[:, cs], tb.ap, AF.Exp, [tb], [pt])

                def stageC(it, idx):
                    C, kt, q0, W, first, last = it
                    c0 = q0 - 512 * C
                    nb, db = 2 + C % 2, 4 + C % 2
                    for h2 in range(2):
                        pt = Pt[(idx % 3) * 2 + h2]
                        R = slice(h2 * 64, h2 * 64 + 64)
                        MM(ps[R, nb, c0:c0 + W], d["v"].ap[:, kt, h2 * 64:h2 * 64 + 64], pt.ap[:, c0:c0 + W], first, last,
                           [d["v"].t(kt * 128, kt * 128 + 128), pt], [BK(nb, 0, h2)])
                    for h2 in range(2):
                        pt = Pt[(idx % 3) * 2 + h2]
                        R = slice(h2 * 64, h2 * 64 + 64)
                        MM(ps[R, db, c0:c0 + W], onesb[:, 0:64], pt.ap[:, c0:c0 + W], first, last, [cT["ones"], pt], [BK(db, 0, h2)])
                    if last:
                        rd = rden[C % 2]
                        VOP(lambda e, db=db, rd=rd: e.reciprocal(out=rd.ap[:, :], in_=ps[:, db, :]), [BK(db)], [rd])
                        TT(d["o"].ap[:, 512 * C:512 * C + 512], ps[:, nb, :], rd.ap[:, :], ALU.mult,
                           [BK(nb), rd], [d["o"].t(512 * C, 512 * C + 512)])

                ni = len(items)
                stageA(items[0], 0)
                for i in range(ni + 1):
                    if i + 1 < ni:
                        stageA(items[i + 1], i + 1)
                    if 0 <= i - 1 < ni:
                        stageC(items[i - 1], i - 1)
                    if i < ni:
                        stageB(items[i], i)


            def sb_core(p):
                d = sets[p % 2]
                items = []
                for c4 in range(4):
                    nk = 4 * c4 + 4
                    for i in range(nk - 1, -1, -1):
                        items.append((c4, i, i == nk - 1, i == 0))
                dc = [0]
                ZB = ((0, 7), (3, 4))
                TBK = (1, 2)
                RR = (slice(0, 64), slice(64, 128))

                TRI = Mx[:, 384:512]

                def cz(it):
                    c4, i, first, last = it
                    r = i - 4 * c4
                    return 128 * max(r, 0), r

                def st1(it, n):
                    c4, i, first, last = it
                    c0, r = cz(it)
                    for h2 in range(2):
                        z = ZB[n % 2][h2]
                        MM(ps[:, z, c0:512], d["k"].ap[RR[h2], i * 128:(i + 1) * 128], d["q"].ap[RR[h2], c4 * 512 + c0:(c4 + 1) * 512], True, True,
                           [d["k"].t(i * 128, i * 128 + 128), d["q"].t(c4 * 512 + c0, c4 * 512 + 512)], [BK(z)])

                def st2a(it, n):
                    c4, i, first, last = it
                    c0, r = cz(it)
                    for h2 in range(2):
                        z = ZB[n % 2][h2]
                        e_ = eb[(n % 2) * 2 + h2]
                        ACT(e_.ap[:, c0:512], ps[:, z, c0:512], AF.Exp, [BK(z)], [e_], scale=-1.0)
                    for h2 in range(2):
                        e_ = eb[(n % 2) * 2 + h2]
                        m = (n % 2) * 2 + h2
                        ACT(spm[m].ap[:, c0:512], e_.ap[:, c0:512], AF.Ln, [e_], [spm[m]], bias=1.0)

                def st2b(it, n, h2):
                    c4, i, first, last = it
                    c0, r = cz(it)
                    z = ZB[n % 2][h2]
                    m = (n % 2) * 2 + h2
                    TT(spp[m].ap[:, c0:512], ps[:, z, c0:512], spm[m].ap[:, c0:512], ALU.add, [BK(z), spm[m]], [spp[m]])
                    if r >= 0:
                        TT(spp[m].ap[:, c0:c0 + 128], spp[m].ap[:, c0:c0 + 128], TRI, ALU.mult, [spp[m], cT["Mr"]], [spp[m]])
                    MM(ps[:, TBK[h2], c0:512], Um[:], spp[m].ap[:, c0:512], first, last, [cT["U"], spp[m]], [BK(TBK[h2])])

                def st3(it, n, h2):
                    c4, i, first, last = it
                    c0, r = cz(it)
                    m = (n % 2) * 2 + h2
                    TT(wb[m].ap[:, c0:512], ps[:, TBK[h2], c0:512], spm[m].ap[:, c0:512], ALU.add, [BK(TBK[h2]), spm[m]], [wb[m]])
                    if not last:
                        MM(ps[:, TBK[h2], c0:512], Lm[:], spp[m].ap[:, c0:512], False, False, [cT["L"], spp[m]], [BK(TBK[h2])])
                    a_ = ab[m]
                    ACT(a_.ap[:, c0:512], wb[m].ap[:, c0:512], AF.Exp, [wb[m]], [a_], scale=-1.0)
                    if r >= 0:
                        TT(a_.ap[:, c0:c0 + 128], a_.ap[:, c0:c0 + 128], TRI, ALU.mult, [a_, cT["Mr"]], [a_])

                def st3av(it, n):
                    c4, i, first, last = it
                    c0, r = cz(it)
                    ob = 5 + c4 % 2
                    for h2 in range(2):
                        a_ = ab[(n % 2) * 2 + h2]
                        MM(ps[RR[h2], ob, c0:512], d["v"].ap[:, i, h2 * 64:h2 * 64 + 64], a_.ap[:, c0:512], first, last,
                           [d["v"].t(i * 128, i * 128 + 128), a_], [BK(ob, None, h2)])
                    if last:
                        ACT(d["o"].ap[:, c4 * 512:(c4 + 1) * 512], ps[:, ob, :], AF.Copy, [BK(ob)],
                            [d["o"].t(c4 * 512, c4 * 512 + 512)])

                n_ = len(items)
                for j in range(n_ + 2):
                    if j < n_:
                        st1(items[j], j)
                    if 0 <= j - 1 < n_:
                        st2a(items[j - 1], j - 1)
                    for h2 in range(2):
                        if 0 <= j - 2 < n_:
                            st3(items[j - 2], j - 2, h2)
                        if 0 <= j - 1 < n_:
                            st2b(items[j - 1], j - 1, h2)
                    if 0 <= j - 2 < n_:
                        st3av(items[j - 2], j - 2)

            load_weights(0)
            for p in range(8):
                if stop_after == "lw":
                    return
                if p + 1 < 8:
                    load_weights(p + 1)
                project(p)
                if stop_after in ("proj", "pj1", "pj2", "pj3", "pj4"):
                    return
                if moba:
                    moba_select(p)
                    if stop_after == "sel":
                        return
                    moba_core(p)
                else:
                    sb_core(p)
                if stop_after == "core":
                    return
                if p % 2 == 1:
                    out_proj2(p - 1)
                if stop_after == "op0":
                    return

        def ffn(l):
            arena.off = 0
            gT = [arena.alloc(4 * S, BF16, shape=[4, S]) for _ in range(2)]
            wu = [arena.alloc(8 * 1024, BF16, shape=[8, 1024]) for _ in range(2)]
            wd = [arena.alloc(4 * 1024, BF16, shape=[4, 1024]) for _ in range(2)]
            tg = [[arena.alloc(512, F32) for _ in range(2)] for _ in range(2)]
            tv = [[arena.alloc(512, F32) for _ in range(2)] for _ in range(2)]
            sg = [arena.alloc(512, F32) for _ in range(2)]
            chunks = [(0, 410), (410, 410), (820, 410), (1230, 410), (1640, 408)]

            def loadw(G):
                nt = 4 if G < 5 else 2
                DMA("pool", wu[G % 2].ap, wup_d[l, G].rearrange("r (c n) -> r c n", c=8), [], [wu[G % 2]], max_dma_last_dim=4096)
                DMA("pool", wd[G % 2].ap[:, 0:nt, :], wdn_d[l, G * 512:G * 512 + nt * 128, :].rearrange("(j r) d -> r j d", r=128),
                    [], [wd[G % 2]], max_dma_last_dim=4096)

            loadw(0)
            kk = 0
            dn = 0
            for G in range(6):
                nt = 4 if G < 5 else 2
                if G + 1 < 6:
                    loadw(G + 1)
                g_ = gT[G % 2]
                wu_ = wu[G % 2]
                wd_ = wd[G % 2]
                for jj in range(nt):
                    j = 4 * G + jj
                    jv = 22 + j
                    for (s0, n) in chunks:
                        st = kk % 2
                        kk += 1
                        gb, vb = 2 * st, 2 * st + 1
                        for c in range(8):
                            MM(ps[:, gb, 0:n + 2], wu_.ap[:, c, jj * 128:(jj + 1) * 128], yrhs(c, s0, n + 2), c == 0, c == 7,
                               [wu_, yts(c, s0, s0 + n + 2), cT["ypad"]], [bankT[gb]])
                        for c in range(8):
                            MM(ps[:, vb, 0:n + 2], wu_.ap[:, c, 512 + jj * 128:512 + (jj + 1) * 128], yrhs(c, s0, n + 2), c == 0, c == 7,
                               [wu_, yts(c, s0, s0 + n + 2), cT["ypad"]], [bankT[vb]])
                        a0, a1 = tg[st]
                        v0, v1 = tv[st]
                        cw = lambda k_, jx: convp[:, l, k_, jx:jx + 1]
                        ACT(a0.ap[:, 0:n], ps[:, gb, 2:n + 2], AF.Identity, [bankT[gb], cT["convp"]], [a0], scale=cw(2, j), bias=cw(3, j))
                        ACT(v0.ap[:, 0:n], ps[:, vb, 2:n + 2], AF.Identity, [bankT[vb], cT["convp"]], [v0], scale=cw(2, jv), bias=cw(3, jv))
                        STT(a1.ap[:, 0:n], ps[:, gb, 1:n + 1], cw(1, j), a0.ap[:, 0:n], ALU.mult, ALU.add, [bankT[gb], a0, cT["convp"]], [a1])
                        STT(a0.ap[:, 0:n], ps[:, gb, 0:n], cw(0, j), a1.ap[:, 0:n], ALU.mult, ALU.add, [bankT[gb], a1, cT["convp"]], [a0])
                        STT(v1.ap[:, 0:n], ps[:, vb, 1:n + 1], cw(1, jv), v0.ap[:, 0:n], ALU.mult, ALU.add, [bankT[vb], v0, cT["convp"]], [v1])
                        STT(v0.ap[:, 0:n], ps[:, vb, 0:n], cw(0, jv), v1.ap[:, 0:n], ALU.mult, ALU.add, [bankT[vb], v1, cT["convp"]], [v0])
                        ACT(sg[st].ap[:, 0:n], a0.ap[:, 0:n], AF.Silu, [a0], [sg[st]])
                        PTT(g_.ap[:, jj, s0:s0 + n], sg[st].ap[:, 0:n], v0.ap[:, 0:n], ALU.mult, [sg[st], v0],
                           [g_.t(jj * S + s0, jj * S + s0 + n)])
                for tc in range(4):
                    for dt_ in range(8):
                        sl = slice(tc * 512, (tc + 1) * 512)
                        b = 4 + dn % 2
                        dn += 1
                        for jj in range(nt):
                            MM(ps[:, b, :], wd_.ap[:, jj, dt_ * 128:(dt_ + 1) * 128], g_.ap[:, jj, sl], jj == 0, jj == nt - 1,
                               [wd_, g_.t(jj * S + tc * 512, jj * S + tc * 512 + 512)], [bankT[b]])
                        TT(hT[:, dt_, sl], ps[:, b, :], hT[:, dt_, sl], ALU.add, [bankT[b], hT_T[dt_][tc]], [hT_T[dt_][tc]])

        done = False
        for l in range(nlayers):
            if stop_after == "const":
                dump_h()
                done = True
                break
            norm(2 * l)
            if stop_after == "norm":
                dump_h()
                done = True
                break
            attention(l)
            if stop_after in ("attn%d" % l, "lw", "proj", "sel", "core", "op0", "pj1", "pj2", "pj3", "pj4"):
                dump_h()
                done = True
                break
            norm(2 * l + 1)
            ffn(l)
            if stop_after == "ffn%d" % l:
                dump_h()
                done = True
                break
        if not done:
            norm(4, final=True)

        P.finalize()
        sems = {e: es.enter_context(nc.semaphore("s_" + e)) for e in COMPUTE}
        dsems = {(q, k): es.enter_context(nc.semaphore("d_%s%d" % (q, k))) for q in QUEUES for k in range(NDMASEM)}
        block = es.enter_context(nc.Block())
        P.emit(block, sems, dsems, out_final)
    return nc, len(P.ops)


def _rel_bucket_np(dist):
    n = np.maximum(dist, 0)
    max_exact = 16
    nf = np.maximum(n, 1).astype(np.float32)
    large = max_exact + (np.log(nf / np.float32(max_exact)) / np.float32(np.log(128 / 16)) * np.float32(16)).astype(np.int32)
    large = np.minimum(large, 31)
    return np.where(n < max_exact, n, large)


def _constants():
    bf = ml_dtypes.bfloat16
    j = np.arange(128)[:, None]
    s = np.arange(128)[None, :]
    U = (j > s).astype(bf)
    L = (j <= s).astype(bf)
    xx = np.arange(896)[None, :]
    Mr = (j < xx - 384).astype(np.float32)
    gmask = np.zeros((128, 16, 8), np.float32)
    for s_ in range(16):
        qb = 4 + (s_ % 8) // 2
        gmask[:, s_, qb:] = -1e30
    koh = np.zeros((8, S), np.float32)
    for n in range(8):
        koh[n, n * 256:(n + 1) * 256] = 1.0
    return dict(idb=np.eye(128).astype(bf), Umat=U, Lmat=L,
                Mr=Mr.astype(bf), koh=koh.astype(bf), gmask=gmask.reshape(128, 128))


def _layout_inputs(x, attn_norm, w_qkv, w_o, rel_bias, ffn_norm, w_up, conv_w, conv_b, w_down, final_norm):
    f = np.float32
    x = np.asarray(x, f)
    vecs = [attn_norm[0], ffn_norm[0], attn_norm[1], ffn_norm[1], final_norm]
    gains = np.stack([np.asarray(v, f).reshape(8, 128).T for v in vecs], axis=1).reshape(128, 40)
    w_qkv = np.asarray(w_qkv, f)
    wq = w_qkv.reshape(2, 8, 128, 3, 8, 128)
    wqkv = np.ascontiguousarray(wq.transpose(0, 4, 2, 1, 3, 5)).reshape(2, 8, 128, 8 * 384)
    w_up = np.asarray(w_up, f)
    wup = np.zeros((2, 6, 128, 8, 1024), f)
    for G in range(6):
        nt = 4 if G < 5 else 2
        g = w_up[:, :, G * 512:G * 512 + nt * 128].reshape(2, 8, 128, nt * 128).transpose(0, 2, 1, 3)
        v = w_up[:, :, DFF + G * 512:DFF + G * 512 + nt * 128].reshape(2, 8, 128, nt * 128).transpose(0, 2, 1, 3)
        wup[:, G, :, :, 0:nt * 128] = g
        wup[:, G, :, :, 512:512 + nt * 128] = v
    wup = wup.reshape(2, 6, 128, 8 * 1024)
    cw = np.asarray(conv_w, f)
    cbv = np.asarray(conv_b, f)
    cp = np.concatenate([cw, cbv[:, None, :]], axis=1)
    convp = np.ascontiguousarray(cp.reshape(2, 4, 44, 128).transpose(3, 0, 1, 2)).reshape(128, 2 * 4 * 44)
    common = dict(gains=np.ascontiguousarray(gains), wqkv=wqkv, wo=np.ascontiguousarray(np.asarray(w_o, f)), wup=wup,
                  wdn=np.ascontiguousarray(np.asarray(w_down, f)), convp=convp,
                  relT=np.ascontiguousarray(np.asarray(rel_bias, f).T))
    common.update(_constants())
    kl = np.arange(128)[:, None]
    xx = np.arange(256)[None, :]
    dd = xx - kl
    gidx = np.where(dd >= 0, _rel_bucket_np(dd), 32)
    rb_ext = np.concatenate([np.asarray(rel_bias, f), np.full((16, 1), NEGV, f)], axis=1)
    common["gtab"] = np.ascontiguousarray(rb_ext[:, gidx])
    maps = []
    for b in range(8):
        m = dict(common)
        m["xT"] = np.ascontiguousarray(x[b].T)
        maps.append(m)
    return maps


_NC_CACHE = {}


def kernel(x, attn_norm, w_qkv, w_o, rel_bias, ffn_norm, w_up, conv_w, conv_b, w_down, final_norm, _stop_after=None, _nlayers=2, _ncores=8):
    key = (_stop_after, _nlayers)
    if key not in _NC_CACHE:
        _NC_CACHE[key] = build(_nlayers, _stop_after)[0]
    nc = _NC_CACHE[key]
    maps = _layout_inputs(x, attn_norm, w_qkv, w_o, rel_bias, ffn_norm, w_up, conv_w, conv_b, w_down, final_norm)
    if _ncores < 8:
        res = run_bass_kernel_spmd(nc, maps[:_ncores], core_ids=list(range(_ncores)))
        return np.stack([np.asarray(r["outT"]).T for r in res.results], axis=0)
    res = run_bass_kernel_spmd(nc, maps, core_ids=list(range(8)))
    out = np.stack([np.asarray(r["outT"]).T for r in res.results], axis=0)
    return np.ascontiguousarray(out.astype(np.float32))
```

```python
import contextlib
import numpy as np
import ml_dtypes
import concourse.bass as bass
import concourse.mybir as mybir
from concourse.bass_utils import run_bass_kernel_spmd

F32 = mybir.dt.float32
BF16 = mybir.dt.bfloat16
F32R = mybir.dt.float32r
AF = mybir.ActivationFunctionType
ALU = mybir.AluOpType
AX = mybir.AxisListType

S = 2048
D = 1024
DFF = 2816
NEGV = -30000.0
COMPUTE = ("pe", "act", "dve", "pool")
QUEUES = ("sp", "pool")
NDMASEM = 12


class T:
    __slots__ = ("w", "rs")

    def __init__(self):
        self.w = None
        self.rs = []


class TP(T):
    __slots__ = ()


def _flat(xs):
    out = []
    for x in xs:
        if isinstance(x, T):
            out.append(x)
        elif isinstance(x, (list, tuple)):
            out.extend(_flat(x))
        else:
            out.extend(x.ts)
    return out


class Prog:
    def __init__(self):
        self.ops = []
        self.dma_rr = {q: 0 for q in QUEUES}
        self.dma_tot = {}

    def op(self, eng, fn, r=(), w=(), dma=False):
        i = len(self.ops)
        r = _flat(r)
        w = _flat(w)
        w = w + [b for b in r if isinstance(b, TP)]
        r = [b for b in r if not isinstance(b, TP)]
        deps = {}
        for b in r:
            if b.w is not None:
                deps[b.w] = "raw"
        for b in w:
            if b.w is not None:
                deps.setdefault(b.w, "waw")
            for x in b.rs:
                deps.setdefault(x, "war")
        for b in r:
            b.rs.append(i)
        for b in w:
            b.w = i
            b.rs = []
        deps.pop(i, None)
        o = dict(eng=eng, fn=fn, deps=deps, dma=dma, sig=False, cnt=None, semi=None, semv=None, prev=None)
        if dma:
            k = self.dma_rr[eng]
            self.dma_rr[eng] = (k + 1) % NDMASEM
            key = (eng, k)
            o["prev"] = self.dma_tot.get(key, 0)
            self.dma_tot[key] = o["prev"] + 16
            o["semi"] = key
            o["semv"] = o["prev"] + 16
        self.ops.append(o)
        return i

    def finalize(self):
        ops = self.ops
        for o in ops:
            need = {}
            for d, kind in o["deps"].items():
                p = ops[d]
                if p["dma"]:
                    need[d] = kind
                    continue
                if p["eng"] == o["eng"] and not o["dma"]:
                    if o["eng"] == "pe" or kind != "raw":
                        continue
                need[d] = kind
                p["sig"] = True
            o["need"] = need
        cnt = {e: 0 for e in set(COMPUTE + QUEUES)}
        for o in ops:
            if o["sig"] and not o["dma"]:
                cnt[o["eng"]] += 1
                o["cnt"] = cnt[o["eng"]]

    def emit(self, block, sems, dsems, out_final):
        ops = self.ops
        engs = {"pe": block.tensor, "act": block.scalar, "dve": block.vector, "pool": block.gpsimd, "sp": block.sync}
        for ename, deco in engs.items():
            mine = [o for o in ops if o["eng"] == ename]

            def body(e, mine=mine, ename=ename):
                waited = {}
                for o in mine:
                    for d in o["need"]:
                        p = ops[d]
                        if p["dma"]:
                            key, v, s = ("d",) + p["semi"], p["semv"], dsems[p["semi"]]
                        else:
                            key, v, s = p["eng"], p["cnt"], sems[p["eng"]]
                        if waited.get(key, 0) >= v:
                            continue
                        waited[key] = v
                        e.wait_ge(s, v)
                    if o["dma"]:
                        key = ("d",) + o["semi"]
                        if o["prev"] > 0 and waited.get(key, 0) < o["prev"]:
                            e.wait_ge(dsems[o["semi"]], o["prev"])
                            waited[key] = o["prev"]
                    ins = o["fn"](e)
                    if o["dma"]:
                        ins.then_inc(dsems[o["semi"]], 16)
                    elif o["sig"]:
                        ins.then_inc(sems[o["eng"]], 1)
                if ename == "sp":
                    for d in out_final:
                        p = ops[d]
                        e.wait_ge(dsems[p["semi"]], p["semv"])

            deco(body)


class Buf:
    def __init__(self, ap, ts, off, slot, esz):
        self.ap = ap
        self.ts = ts
        self.off = off
        self.slot = slot
        self.esz = esz

    def t(self, lo, hi):
        a = (self.off + lo * self.esz) // self.slot
        b = (self.off + hi * self.esz - 1) // self.slot
        base = self.off // self.slot
        return self.ts[a - base:b - base + 1]


class Arena:
    SLOT = 512

    def __init__(self, ap_bf16, nbytes):
        self.ap = ap_bf16
        self.n = nbytes
        self.T = [T() for _ in range(nbytes // self.SLOT)]
        self.off = 0

    def alloc(self, nelem, dt=BF16, parts=128, shape=None):
        esz = 4 if dt in (F32, F32R) else 2
        nb = nelem * esz
        nb = (nb + self.SLOT - 1) // self.SLOT * self.SLOT
        off = self.off
        self.off += nb
        assert self.off <= self.n, ("arena overflow", self.off, self.n)
        v = self.ap[0:parts, off // 2:(off + nelem * esz) // 2]
        if dt != BF16:
            v = v.bitcast(dt)
        if shape is not None:
            names = " ".join("abcdef"[i] for i in range(len(shape)))
            v = v.rearrange("p (%s) -> p %s" % (names, names), **{"abcdef"[i]: shape[i] for i in range(len(shape))})
        ts = self.T[off // self.SLOT:(off + nb) // self.SLOT]
        return Buf(v, ts, off, self.SLOT, esz)


def build(nlayers=2, stop_after=None):
    nc = bass.Bass("TRN2", target_bir_lowering=False)

    def DI(name, shape, dt=F32):
        return nc.dram_tensor(name, shape, dt, kind="ExternalInput").ap()

    xT_d = DI("xT", [D, S])
    gains_d = DI("gains", [128, 40])
    wqkv_d = DI("wqkv", [2, 8, 128, 8 * 384])
    wo_d = DI("wo", [2, D, D])
    wup_d = DI("wup", [2, 6, 128, 8 * 1024])
    wdn_d = DI("wdn", [2, DFF, D])
    convp_d = DI("convp", [128, 2 * 4 * 44])
    relT_d = DI("relT", [32, 16])
    idb_d = DI("idb", [128, 128], BF16)
    U_d = DI("Umat", [128, 128], BF16)
    L_d = DI("Lmat", [128, 128], BF16)
    Mr_d = DI("Mr", [128, 896], BF16)
    koh_d = DI("koh", [8, S], BF16)
    gmask_d = DI("gmask", [128, 128])
    outT_d = nc.dram_tensor("outT", [D, S], F32, kind="ExternalOutput").ap()
    gscr_d = DI("gtab", [16, 128, 256])

    P = Prog()
    ARENA_BYTES = 100 * 1024
    es = contextlib.ExitStack()
    with es:
        hT = es.enter_context(nc.sbuf_tensor("hT", [128, 8, S], F32))
        yT = es.enter_context(nc.sbuf_tensor("yT", [128, 8, S + 4], BF16))
        arena_t = es.enter_context(nc.sbuf_tensor("arena", [128, ARENA_BYTES // 2], BF16))
        gains = es.enter_context(nc.sbuf_tensor("gains_sb", [128, 40], F32))
        convp = es.enter_context(nc.sbuf_tensor("convp_sb", [128, 2, 4, 44], F32))
        idb = es.enter_context(nc.sbuf_tensor("idb_sb", [128, 128], BF16))
        Um = es.enter_context(nc.sbuf_tensor("U_sb", [128, 128], BF16))
        Lm = es.enter_context(nc.sbuf_tensor("L_sb", [128, 128], BF16))
        Mx = es.enter_context(nc.sbuf_tensor("Mr_sb", [128, 896], BF16))
        onesf = es.enter_context(nc.sbuf_tensor("onesf", [128, 128], F32))
        onesr = es.enter_context(nc.sbuf_tensor("onesr", [128, 128], F32R))
        onesb = es.enter_context(nc.sbuf_tensor("onesb", [128, 64], BF16))
        sqr = es.enter_context(nc.sbuf_tensor("sqr", [128, 2, 512], F32R))
        sqT = [T(), T()]
        cb = es.enter_context(nc.sbuf_tensor("cb", [128, 16], F32))
        kms = es.enter_context(nc.sbuf_tensor("kms", [128, 8], F32))
        kmb = es.enter_context(nc.sbuf_tensor("kmb", [128, 8], BF16))
        ps = es.enter_context(nc.psum_tensor("ps", [128, 8, 512], F32))
        psb = ps[:, 7, :].bitcast(BF16)

        arena = Arena(arena_t[:], ARENA_BYTES)
        hT_T = [[T() for _ in range(4)] for _ in range(8)]
        yT_T = [[T() for _ in range(4)] for _ in range(8)]
        bankQ = [[TP(), TP()] for _ in range(8)]

        def BK(b, half=None, rows=None):
            rs = [0, 1] if rows is None else [rows]
            return [bankQ[b][r] for r in rs]

        class _BT:
            def __getitem__(self, b):
                return BK(b)
        bankT = _BT()
        cT = {k: T() for k in ("gains", "convp", "idb", "U", "L", "Mr", "ones", "cb", "relT", "onehot", "negm", "Fsb", "fscr",
                               "g8", "top8", "npad0", "npad1", "kms", "kmb", "ypad")}

        def MM(out, lhsT, rhs, start, stop, r, w):
            P.op("pe", lambda e: e.matmul(out, lhsT=lhsT, rhs=rhs, start=start, stop=stop), r=r, w=w)

        def ACT(out, in_, func, r, w, **kw):
            P.op("act", lambda e: e.activation(out=out, in_=in_, func=func, **kw), r=r, w=w)

        def TT(out, in0, in1, op, r, w):
            P.op("dve", lambda e: e.tensor_tensor(out=out, in0=in0, in1=in1, op=op), r=r, w=w)

        def TS(out, in0, s1, s2, op0, op1, r, w):
            P.op("dve", lambda e: e.tensor_scalar(out=out, in0=in0, scalar1=s1, scalar2=s2, op0=op0, op1=op1), r=r, w=w)

        def STT(out, in0, scalar, in1, op0, op1, r, w):
            P.op("dve", lambda e: e.scalar_tensor_tensor(out=out, in0=in0, scalar=scalar, in1=in1, op0=op0, op1=op1), r=r, w=w)

        def PTT(out, in0, in1, op, r, w):
            P.op("pool", lambda e: e.tensor_tensor(out=out, in0=in0, in1=in1, op=op), r=r, w=w)

        def VCOPY(out, in_, r, w):
            P.op("dve", lambda e: e.tensor_copy(out=out, in_=in_), r=r, w=w)

        def VOP(fn, r, w):
            P.op("dve", fn, r=r, w=w)

        def DMA(q, out, in_, r, w, **kw):
            return P.op(q, lambda e: e.dma_start(out=out, in_=in_, **kw), r=r, w=w, dma=True)

        DMA("sp", gains[:], gains_d[:], [], [cT["gains"]])
        DMA("sp", convp[:].rearrange("p a b c -> p (a b c)"), convp_d[:], [], [cT["convp"]])
        DMA("sp", idb[:], idb_d[:], [], [cT["idb"]])
        DMA("sp", Um[:], U_d[:], [], [cT["U"]])
        DMA("sp", Lm[:], L_d[:], [], [cT["L"]])
        DMA("sp", Mx[:], Mr_d[:], [], [cT["Mr"]])
        DMA("sp", cb[:], relT_d[31:32, :].partition_broadcast(128), [], [cT["cb"]])
        for tc in range(4):
            for c in range(8):
                DMA("sp", hT[:, c, tc * 512:(tc + 1) * 512], xT_d[c * 128:(c + 1) * 128, tc * 512:(tc + 1) * 512], [], [hT_T[c][tc]])
        VOP(lambda e: e.memset(onesf[:], 1.0), [], [cT["ones"]])
        VCOPY(onesr[:], onesf[:], [cT["ones"]], [cT["ones"]])
        VCOPY(onesb[:], onesf[:, 0:64], [cT["ones"]], [cT["ones"]])
        VOP(lambda e: e.memset(yT[:, :, 0:2], 0.0), [], [cT["ypad"]])
        out_final = []

        def norm(gi, final=False):
            arena.off = 0
            lnt = [arena.alloc(512, F32) for _ in range(2)]
            stg = [arena.alloc(512, F32) for _ in range(3)] if final else None
            k = 0
            for tc in range(4):
                sl = slice(tc * 512, (tc + 1) * 512)
                b = 5 + (tc % 2)
                for c in range(8):
                    sq = sqT[k % 2]
                    sqa = sqr[:, k % 2, :]
                    k += 1
                    ACT(sqa, hT[:, c, sl], AF.Square, [hT_T[c][tc]], [sq])
                    MM(ps[:, b, :], onesr[:], sqa, c == 0, c == 7, [sq, cT["ones"]], [bankT[b]])
                lt = lnt[tc % 2]
                ACT(lt.ap, ps[:, b, :], AF.Ln, [bankT[b]], [lt], scale=1.0 / D, bias=1e-6)
                ACT(ps[:, b, :], lt.ap, AF.Exp, [lt], [bankT[b]], scale=-0.5)
                for c in range(8):
                    if not final:
                        STT(yT[:, c, 2 + tc * 512:2 + (tc + 1) * 512], hT[:, c, sl], gains[:, gi * 8 + c:gi * 8 + c + 1], ps[:, b, :],
                            ALU.mult, ALU.mult, [hT_T[c][tc], cT["gains"], bankT[b], cT["ypad"]], [yT_T[c][tc]])
                    else:
                        st = stg[(tc * 8 + c) % 3]
                        STT(st.ap, hT[:, c, sl], gains[:, gi * 8 + c:gi * 8 + c + 1], ps[:, b, :],
                            ALU.mult, ALU.mult, [hT_T[c][tc], cT["gains"], bankT[b]], [st])
                        out_final.append(DMA("sp", outT_d[c * 128:(c + 1) * 128, sl], st.ap, [st], []))

        def dump_h():
            for c in range(8):
                out_final.append(DMA("sp", outT_d[c * 128:(c + 1) * 128, :], hT[:, c, :], [hT_T[c]], []))

        def yrhs(c, lo, n):
            return yT[:, c, lo:lo + n]

        def yts(c, lo, hi):
            a = max(0, (lo - 2)) // 512
            b = min(3, max(0, hi - 3) // 512)
            return [yT_T[c][i] for i in range(a, b + 1)]

        def attention(l):
            moba = (l % 2 == 0)
            arena.off = 0
            sets = []
            for s_ in range(2):
                d = {}
                d["q"] = arena.alloc(S, BF16)
                d["k"] = arena.alloc(S, BF16)
                if moba:
                    d["qx"] = arena.alloc(S, BF16)
                    d["kx"] = arena.alloc(S, BF16)
                    d["Tb"] = arena.alloc(4 * 128, F32, shape=[2, 2, 128])
                d["v"] = arena.alloc(16 * 128, BF16, shape=[16, 128])
                d["o"] = arena.alloc(S, BF16)
                d["wq"] = arena.alloc(8 * 384, BF16, shape=[8, 384])
                sets.append(d)
            wos = [arena.alloc(D, BF16) for _ in range(4)]
            if moba:
                Pt = [arena.alloc(512, BF16) for _ in range(6)]
                g16b = arena.alloc(128, F32, shape=[16, 8])
                top16b = arena.alloc(128, F32, shape=[16, 8])
                npad16b = arena.alloc(16 * 72, BF16, shape=[16, 72])
                gmaskb = arena.alloc(128, F32)
                g16, top16, npad16, gmask = g16b.ap, top16b.ap, npad16b.ap, gmaskb.ap
                VOP(lambda e: e.memset(npad16, 0.0), [], [npad16b])
                DMA("sp", gmask, gmask_d[:], [], [gmaskb])
                tmpb = [arena.alloc(128, F32) for _ in range(4)]
                rden = [arena.alloc(512, F32) for _ in range(2)]
            else:
                eb = [arena.alloc(512, F32) for _ in range(4)]
                spm = [arena.alloc(512, F32) for _ in range(4)]
                spp = [arena.alloc(512, BF16) for _ in range(4)]
                wb = [arena.alloc(512, F32) for _ in range(4)]
                ab = [arena.alloc(512, BF16) for _ in range(4)]
                tmpd = [arena.alloc(512, F32) for _ in range(4)]

            def load_weights(p):
                d = sets[p % 2]
                DMA("pool", d["wq"].ap, wqkv_d[l, p].rearrange("r (c n) -> r c n", c=8), [], [d["wq"]], max_dma_last_dim=4096)
                DMA("pool", wos[p % 4].ap, wo_d[l, p * 128:(p + 1) * 128, :], [], [wos[p % 4]], max_dma_last_dim=4096)
                if moba:
                    for h2 in range(2):
                        DMA("sp", d["kx"].ap[h2 * 64:h2 * 64 + 8, :], koh_d[:, :], [], [d["kx"]])
                        h = 2 * p + h2
                        DMA("sp", d["Tb"].ap[:, h2, :, :], gscr_d[h].rearrange("k (w q) -> k w q", w=2), [], [d["Tb"]])

            pj_rot = [6, 0, 1] if moba else [0, 7, 3]
            pjc = [0]

            def pjbank():
                b = pj_rot[pjc[0] % 3]
                pjc[0] += 1
                return b

            def project(p):
                d = sets[p % 2]
                wq = d["wq"]
                for tc in range(4):
                    sl = slice(tc * 512, (tc + 1) * 512)
                    b = pjbank()
                    for c in range(8):
                        MM(ps[:, b, :], wq.ap[:, c, 0:128], yrhs(c, 2 + tc * 512, 512), c == 0, c == 7, [wq, yT_T[c][tc]], [bankT[b]])
                    ACT(d["q"].ap[:, sl], ps[:, b, :], AF.Copy, [bankT[b]], [d["q"].t(tc * 512, tc * 512 + 512)], scale=0.125)
                    b = pjbank()
                    for c in range(8):
                        MM(ps[:, b, :], wq.ap[:, c, 128:256], yrhs(c, 2 + tc * 512, 512), c == 0, c == 7, [wq, yT_T[c][tc]], [bankT[b]])
                    if moba:
                        for hf in range(2):
                            cs = slice(tc * 512 + hf * 256, tc * 512 + hf * 256 + 256)
                            ACT(d["k"].ap[:, cs], ps[:, b, hf * 256:hf * 256 + 256], AF.Copy, [bankT[b]],
                                [d["k"].t(cs.start, cs.stop), cT["kms"]], accum_out=kms[:, tc * 2 + hf:tc * 2 + hf + 1])
                    else:
                        VCOPY(d["k"].ap[:, sl], ps[:, b, :], [bankT[b]], [d["k"].t(tc * 512, tc * 512 + 512)])
                if moba:
                    ACT(kmb[:, :], kms[:, :], AF.Copy, [cT["kms"]], [cT["kmb"]], scale=1.0 / 256)
                for tt in range(16):
                    b = pjbank()
                    for c in range(8):
                        MM(ps[:, b, 0:128], yrhs(c, 2 + tt * 128, 128), wq.ap[:, c, 256:384], c == 0, c == 7, [wq, yT_T[c][tt // 4]], [bankT[b]])
                    if tt % 2 == 0:
                        VCOPY(d["v"].ap[:, tt, :], ps[:, b, 0:128], [bankT[b]], [d["v"].t(tt * 128, tt * 128 + 128)])
                    else:
                        ACT(d["v"].ap[:, tt, :], ps[:, b, 0:128], AF.Copy, [bankT[b]], [d["v"].t(tt * 128, tt * 128 + 128)])

            def out_proj2(p0):
                for tc in range(4):
                    for dt_ in range(8):
                        sl = slice(tc * 512, (tc + 1) * 512)
                        b = pjbank()
                        for k_, p_ in enumerate((p0, p0 + 1)):
                            d = sets[p_ % 2]
                            MM(ps[:, b, :], wos[p_ % 4].ap[:, dt_ * 128:(dt_ + 1) * 128], d["o"].ap[:, sl], k_ == 0, k_ == 1,
                               [wos[p_ % 4], d["o"].t(tc * 512, tc * 512 + 512)], [bankT[b]])
                        TT(hT[:, dt_, sl], ps[:, b, :], hT[:, dt_, sl], ALU.add, [bankT[b], hT_T[dt_][tc]], [hT_T[dt_][tc]])


            def moba_select(p):
                d = sets[p % 2]
                for h2 in range(2):
                    R = slice(h2 * 64, h2 * 64 + 64)
                    for j in range(8):
                        qt = 8 + j
                        c = (h2 * 8 + j) * 8
                        MM(ps[:, 6, c:c + 8], d["q"].ap[R, qt * 128:(qt + 1) * 128], kmb[R, :], True, True,
                           [d["q"].t(qt * 128, qt * 128 + 128), cT["kmb"]], [bankT[6]])
                TT(g16.rearrange("p a b -> p (a b)"), ps[:, 6, 0:128], gmask, ALU.add, [bankT[6], gmaskb], [g16b])
                for s_ in range(16):
                    VOP(lambda e, s_=s_: e.max(out=top16[:, s_, :], in_=g16[:, s_, :]), [g16b], [top16b])
                for s_ in range(16):
                    h2 = s_ // 8
                    TS(npad16[:, s_, h2 * 64:h2 * 64 + 8], g16[:, s_, :], top16[:, s_, 2:3], NEGV, ALU.is_lt, ALU.mult,
                       [g16b, top16b], [npad16b])
                for h2 in range(2):
                    for j in range(8):
                        P.op("pe", lambda e, h2=h2, j=j: e.transpose(out=psb[0:72, j * 128:(j + 1) * 128], in_=npad16[:, h2 * 8 + j, :], identity=idb[:]),
                             r=[npad16b, cT["idb"]], w=[BK(7)])
                    ACT(d["qx"].ap[h2 * 64:h2 * 64 + 8, 1024:2048], psb[h2 * 64:h2 * 64 + 8, 0:1024], AF.Copy, [BK(7)],
                        [d["qx"].t(1024, 2048)])

            def moba_core(p):
                d = sets[p % 2]
                items = []
                for C in range(4):
                    base = 512 * C
                    its = []
                    for kt in range(4 * C):
                        its.append((kt, base, 512))
                    its.append((4 * C, base, 512))
                    its.append((4 * C + 1, base + 128, 384))
                    its.append((4 * C + 2, base + 256, 256))
                    its.append((4 * C + 3, base + 384, 128))
                    for ii, (kt, q0, W) in enumerate(its):
                        items.append((C, kt, q0, W, ii == 0, ii == len(its) - 1))
                sc = [0]
                SBK = ((0, 1), (6, 7))

                def stageA(it, idx):
                    C, kt, q0, W, first, last = it
                    n = kt // 2
                    c0 = q0 - 512 * C
                    a0 = max(q0, 256 * (n + 1), 1024)
                    for h2 in range(2):
                        R = slice(h2 * 64, h2 * 64 + 64)
                        sb_ = SBK[idx % 2][h2]
                        MM(ps[:, sb_, c0:c0 + W], d["k"].ap[R, kt * 128:(kt + 1) * 128], d["q"].ap[R, q0:q0 + W], True, True,
                           [d["k"].t(kt * 128, kt * 128 + 128), d["q"].t(q0, q0 + W)], [BK(sb_)])
                    if a0 < q0 + W:
                        ca = a0 - 512 * C
                        for h2 in range(2):
                            X = slice(h2 * 64, h2 * 64 + 8)
                            sb_ = SBK[idx % 2][h2]
                            MM(ps[:, sb_, ca:c0 + W], d["kx"].ap[X, kt * 128:(kt + 1) * 128], d["qx"].ap[X, a0:q0 + W], False, True,
                               [d["kx"].t(kt * 128, kt * 128 + 128), d["qx"].t(a0, q0 + W)], [BK(sb_)])

                def stageB(it, idx):
                    C, kt, q0, W, first, last = it
                    c0 = q0 - 512 * C
                    kinds = []
                    for u in range(W // 128):
                        delta = q0 + u * 128 - kt * 128
                        kinds.append(0 if delta == 0 else (1 if delta == 128 else 2))
                    groups = []
                    u = 0
                    while u < len(kinds):
                        if kinds[u] == 2:
                            v_ = u
                            while v_ < len(kinds) and kinds[v_] == 2:
                                v_ += 1
                            groups.append((2, u, v_))
                            u = v_
                        else:
                            groups.append((kinds[u], u, u + 1))
                            u += 1
                    for h2 in range(2):
                        h = 2 * p + h2
                        sb_ = SBK[idx % 2][h2]
                        pt = Pt[(idx % 3) * 2 + h2]
                        for kd, u0, u1 in groups:
                            cs = slice(c0 + u0 * 128, c0 + u1 * 128)
                            if kd == 2:
                                ACT(pt.ap[:, cs], ps[:, sb_, cs], AF.Exp, [BK(sb_), cT["cb"]], [pt], bias=cb[:, h:h + 1], scale=1.0)
                            else:
                                tb = tmpb[sc[0] % 4]
                                sc[0] += 1
                                TT(tb.ap, ps[:, sb_, cs], d["Tb"].ap[:, h2, kd, :], ALU.add, [BK(sb_), d["Tb"]], [tb])
                                ACT(pt.ap[:, cs], tb.ap, AF.Exp, [tb], [pt])

                def stageC(it, idx):
                    C, kt, q0, W, first, last = it
                    c0 = q0 - 512 * C
                    nb, db = 2 + C % 2, 4 + C % 2
                    for h2 in range(2):
                        pt = Pt[(idx % 3) * 2 + h2]
                        R = slice(h2 * 64, h2 * 64 + 64)
                        MM(ps[R, nb, c0:c0 + W], d["v"].ap[:, kt, h2 * 64:h2 * 64 + 64], pt.ap[:, c0:c0 + W], first, last,
                           [d["v"].t(kt * 128, kt * 128 + 128), pt], [BK(nb, 0, h2)])
                    for h2 in range(2):
                        pt = Pt[(idx % 3) * 2 + h2]
                        R = slice(h2 * 64, h2 * 64 + 64)
                        MM(ps[R, db, c0:c0 + W], onesb[:, 0:64], pt.ap[:, c0:c0 + W], first, last, [cT["ones"], pt], [BK(db, 0, h2)])
                    if last:
                        rd = rden[C % 2]
                        VOP(lambda e, db=db, rd=rd: e.reciprocal(out=rd.ap[:, :], in_=ps[:, db, :]), [BK(db)], [rd])
                        TT(d["o"].ap[:, 512 * C:512 * C + 512], ps[:, nb, :], rd.ap[:, :], ALU.mult,
                           [BK(nb), rd], [d["o"].t(512 * C, 512 * C + 512)])

                ni = len(items)
                stageA(items[0], 0)
                for i in range(ni + 1):
                    if i + 1 < ni:
                        stageA(items[i + 1], i + 1)
                    if 0 <= i - 1 < ni:
                        stageC(items[i - 1], i - 1)
                    if i < ni:
                        stageB(items[i], i)


            def sb_core(p):
                d = sets[p % 2]
                items = []
                for c4 in range(4):
                    nk = 4 * c4 + 4
                    for i in range(nk - 1, -1, -1):
                        items.append((c4, i, i == nk - 1, i == 0))
                dc = [0]
                ZB = ((0, 7), (3, 4))
                TBK = (1, 2)
                RR = (slice(0, 64), slice(64, 128))

                TRI = Mx[:, 384:512]

                def cz(it):
                    c4, i, first, last = it
                    r = i - 4 * c4
                    return 128 * max(r, 0), r

                def st1(it, n):
                    c4, i, first, last = it
                    c0, r = cz(it)
                    for h2 in range(2):
                        z = ZB[n % 2][h2]
                        MM(ps[:, z, c0:512], d["k"].ap[RR[h2], i * 128:(i + 1) * 128], d["q"].ap[RR[h2], c4 * 512 + c0:(c4 + 1) * 512], True, True,
                           [d["k"].t(i * 128, i * 128 + 128), d["q"].t(c4 * 512 + c0, c4 * 512 + 512)], [BK(z)])

                def st2a(it, n):
                    c4, i, first, last = it
                    c0, r = cz(it)
                    for h2 in range(2):
                        z = ZB[n % 2][h2]
                        e_ = eb[(n % 2) * 2 + h2]
                        m = (n % 2) * 2 + h2
                        ACT(e_.ap[:, c0:512], ps[:, z, c0:512], AF.Exp, [BK(z)], [e_], scale=-1.0)
                        ACT(spm[m].ap[:, c0:512], e_.ap[:, c0:512], AF.Ln, [e_], [spm[m]], bias=1.0)

                def st2b(it, n, h2):
                    c4, i, first, last = it
                    c0, r = cz(it)
                    z = ZB[n % 2][h2]
                    m = (n % 2) * 2 + h2
                    TT(spp[m].ap[:, c0:512], ps[:, z, c0:512], spm[m].ap[:, c0:512], ALU.add, [BK(z), spm[m]], [spp[m]])
                    if r >= 0:
                        TT(spp[m].ap[:, c0:c0 + 128], spp[m].ap[:, c0:c0 + 128], TRI, ALU.mult, [spp[m], cT["Mr"]], [spp[m]])
                    MM(ps[:, TBK[h2], c0:512], Um[:], spp[m].ap[:, c0:512], first, last, [cT["U"], spp[m]], [BK(TBK[h2])])

                def st3(it, n, h2):
                    c4, i, first, last = it
                    c0, r = cz(it)
                    m = (n % 2) * 2 + h2
                    TT(wb[m].ap[:, c0:512], ps[:, TBK[h2], c0:512], spm[m].ap[:, c0:512], ALU.add, [BK(TBK[h2]), spm[m]], [wb[m]])
                    if not last:
                        MM(ps[:, TBK[h2], c0:512], Lm[:], spp[m].ap[:, c0:512], False, False, [cT["L"], spp[m]], [BK(TBK[h2])])
                    a_ = ab[m]
                    ACT(a_.ap[:, c0:512], wb[m].ap[:, c0:512], AF.Exp, [wb[m]], [a_], scale=-1.0)
                    if r >= 0:
                        TT(a_.ap[:, c0:c0 + 128], a_.ap[:, c0:c0 + 128], TRI, ALU.mult, [a_, cT["Mr"]], [a_])

                def st3av(it, n):
                    c4, i, first, last = it
                    c0, r = cz(it)
                    ob = 5 + c4 % 2
                    for h2 in range(2):
                        a_ = ab[(n % 2) * 2 + h2]
                        MM(ps[RR[h2], ob, c0:512], d["v"].ap[:, i, h2 * 64:h2 * 64 + 64], a_.ap[:, c0:512], first, last,
                           [d["v"].t(i * 128, i * 128 + 128), a_], [BK(ob, None, h2)])
                    if last:
                        VCOPY(d["o"].ap[:, c4 * 512:(c4 + 1) * 512], ps[:, ob, :], [BK(ob)],
                              [d["o"].t(c4 * 512, c4 * 512 + 512)])

                n_ = len(items)
                for j in range(n_ + 2):
                    if j < n_:
                        st1(items[j], j)
                    if 0 <= j - 1 < n_:
                        st2a(items[j - 1], j - 1)
                    for h2 in range(2):
                        if 0 <= j - 2 < n_:
                            st3(items[j - 2], j - 2, h2)
                        if 0 <= j - 1 < n_:
                            st2b(items[j - 1], j - 1, h2)
                    if 0 <= j - 2 < n_:
                        st3av(items[j - 2], j - 2)

            load_weights(0)
            for p in range(8):
                if stop_after == "lw":
                    return
                if p + 1 < 8:
                    load_weights(p + 1)
                project(p)
                if stop_after in ("proj", "pj1", "pj2", "pj3", "pj4"):
                    return
                if moba:
                    moba_select(p)
                    if stop_after == "sel":
                        return
                    moba_core(p)
                else:
                    sb_core(p)
                if stop_after == "core":
                    return
                if p % 2 == 1:
                    out_proj2(p - 1)
                if stop_after == "op0":
                    return

        def ffn(l):
            arena.off = 0
            gT = [arena.alloc(4 * S, BF16, shape=[4, S]) for _ in range(2)]
            wu = [arena.alloc(8 * 1024, BF16, shape=[8, 1024]) for _ in range(2)]
            wd = [arena.alloc(4 * 1024, BF16, shape=[4, 1024]) for _ in range(2)]
            tg = [[arena.alloc(512, F32) for _ in range(2)] for _ in range(2)]
            tv = [[arena.alloc(512, F32) for _ in range(2)] for _ in range(2)]
            sg = [arena.alloc(512, F32) for _ in range(2)]
            chunks = [(0, 410), (410, 410), (820, 410), (1230, 410), (1640, 408)]

            def loadw(G):
                nt = 4 if G < 5 else 2
                DMA("pool", wu[G % 2].ap, wup_d[l, G].rearrange("r (c n) -> r c n", c=8), [], [wu[G % 2]], max_dma_last_dim=4096)
                DMA("pool", wd[G % 2].ap[:, 0:nt, :], wdn_d[l, G * 512:G * 512 + nt * 128, :].rearrange("(j r) d -> r j d", r=128),
                    [], [wd[G % 2]], max_dma_last_dim=4096)

            loadw(0)
            kk = 0
            dn = 0
            for G in range(6):
                nt = 4 if G < 5 else 2
                if G + 1 < 6:
                    loadw(G + 1)
                g_ = gT[G % 2]
                wu_ = wu[G % 2]
                wd_ = wd[G % 2]
                for jj in range(nt):
                    j = 4 * G + jj
                    jv = 22 + j
                    for (s0, n) in chunks:
                        st = kk % 2
                        kk += 1
                        gb, vb = 2 * st, 2 * st + 1
                        for c in range(8):
                            MM(ps[:, gb, 0:n + 2], wu_.ap[:, c, jj * 128:(jj + 1) * 128], yrhs(c, s0, n + 2), c == 0, c == 7,
                               [wu_, yts(c, s0, s0 + n + 2), cT["ypad"]], [bankT[gb]])
                        for c in range(8):
                            MM(ps[:, vb, 0:n + 2], wu_.ap[:, c, 512 + jj * 128:512 + (jj + 1) * 128], yrhs(c, s0, n + 2), c == 0, c == 7,
                               [wu_, yts(c, s0, s0 + n + 2), cT["ypad"]], [bankT[vb]])
                        a0, a1 = tg[st]
                        v0, v1 = tv[st]
                        cw = lambda k_, jx: convp[:, l, k_, jx:jx + 1]
                        ACT(a0.ap[:, 0:n], ps[:, gb, 2:n + 2], AF.Identity, [bankT[gb], cT["convp"]], [a0], scale=cw(2, j), bias=cw(3, j))
                        ACT(v0.ap[:, 0:n], ps[:, vb, 2:n + 2], AF.Identity, [bankT[vb], cT["convp"]], [v0], scale=cw(2, jv), bias=cw(3, jv))
                        STT(a1.ap[:, 0:n], ps[:, gb, 1:n + 1], cw(1, j), a0.ap[:, 0:n], ALU.mult, ALU.add, [bankT[gb], a0, cT["convp"]], [a1])
                        STT(a0.ap[:, 0:n], ps[:, gb, 0:n], cw(0, j), a1.ap[:, 0:n], ALU.mult, ALU.add, [bankT[gb], a1, cT["convp"]], [a0])
                        STT(v1.ap[:, 0:n], ps[:, vb, 1:n + 1], cw(1, jv), v0.ap[:, 0:n], ALU.mult, ALU.add, [bankT[vb], v0, cT["convp"]], [v1])
                        STT(v0.ap[:, 0:n], ps[:, vb, 0:n], cw(0, jv), v1.ap[:, 0:n], ALU.mult, ALU.add, [bankT[vb], v1, cT["convp"]], [v0])
                        ACT(sg[st].ap[:, 0:n], a0.ap[:, 0:n], AF.Silu, [a0], [sg[st]])
                        PTT(g_.ap[:, jj, s0:s0 + n], sg[st].ap[:, 0:n], v0.ap[:, 0:n], ALU.mult, [sg[st], v0],
                           [g_.t(jj * S + s0, jj * S + s0 + n)])
                for tc in range(4):
                    for dt_ in range(8):
                        sl = slice(tc * 512, (tc + 1) * 512)
                        b = 4 + dn % 2
                        dn += 1
                        for jj in range(nt):
                            MM(ps[:, b, :], wd_.ap[:, jj, dt_ * 128:(dt_ + 1) * 128], g_.ap[:, jj, sl], jj == 0, jj == nt - 1,
                               [wd_, g_.t(jj * S + tc * 512, jj * S + tc * 512 + 512)], [bankT[b]])
                        TT(hT[:, dt_, sl], ps[:, b, :], hT[:, dt_, sl], ALU.add, [bankT[b], hT_T[dt_][tc]], [hT_T[dt_][tc]])

        done = False
        for l in range(nlayers):
            if stop_after == "const":
                dump_h()
                done = True
                break
            norm(2 * l)
            if stop_after == "norm":
                dump_h()
                done = True
                break
            attention(l)
            if stop_after in ("attn%d" % l, "lw", "proj", "sel", "core", "op0", "pj1", "pj2", "pj3", "pj4"):
                dump_h()
                done = True
                break
            norm(2 * l + 1)
            ffn(l)
            if stop_after == "ffn%d" % l:
                dump_h()
                done = True
                break
        if not done:
            norm(4, final=True)

        P.finalize()
        sems = {e: es.enter_context(nc.semaphore("s_" + e)) for e in COMPUTE}
        dsems = {(q, k): es.enter_context(nc.semaphore("d_%s%d" % (q, k))) for q in QUEUES for k in range(NDMASEM)}
        block = es.enter_context(nc.Block())
        P.emit(block, sems, dsems, out_final)
    return nc, len(P.ops)


def _rel_bucket_np(dist):
    n = np.maximum(dist, 0)
    max_exact = 16
    nf = np.maximum(n, 1).astype(np.float32)
    large = max_exact + (np.log(nf / np.float32(max_exact)) / np.float32(np.log(128 / 16)) * np.float32(16)).astype(np.int32)
    large = np.minimum(large, 31)
    return np.where(n < max_exact, n, large)


def _constants():
    bf = ml_dtypes.bfloat16
    j = np.arange(128)[:, None]
    s = np.arange(128)[None, :]
    U = (j > s).astype(bf)
    L = (j <= s).astype(bf)
    xx = np.arange(896)[None, :]
    Mr = (j < xx - 384).astype(np.float32)
    gmask = np.zeros((128, 16, 8), np.float32)
    for s_ in range(16):
        qb = 4 + (s_ % 8) // 2
        gmask[:, s_, qb:] = -1e30
    koh = np.zeros((8, S), np.float32)
    for n in range(8):
        koh[n, n * 256:(n + 1) * 256] = 1.0
    return dict(idb=np.eye(128).astype(bf), Umat=U, Lmat=L,
                Mr=Mr.astype(bf), koh=koh.astype(bf), gmask=gmask.reshape(128, 128))


def _layout_inputs(x, attn_norm, w_qkv, w_o, rel_bias, ffn_norm, w_up, conv_w, conv_b, w_down, final_norm):
    f = np.float32
    x = np.asarray(x, f)
    vecs = [attn_norm[0], ffn_norm[0], attn_norm[1], ffn_norm[1], final_norm]
    gains = np.stack([np.asarray(v, f).reshape(8, 128).T for v in vecs], axis=1).reshape(128, 40)
    w_qkv = np.asarray(w_qkv, f)
    wq = w_qkv.reshape(2, 8, 128, 3, 8, 128)
    wqkv = np.ascontiguousarray(wq.transpose(0, 4, 2, 1, 3, 5)).reshape(2, 8, 128, 8 * 384)
    w_up = np.asarray(w_up, f)
    wup = np.zeros((2, 6, 128, 8, 1024), f)
    for G in range(6):
        nt = 4 if G < 5 else 2
        g = w_up[:, :, G * 512:G * 512 + nt * 128].reshape(2, 8, 128, nt * 128).transpose(0, 2, 1, 3)
        v = w_up[:, :, DFF + G * 512:DFF + G * 512 + nt * 128].reshape(2, 8, 128, nt * 128).transpose(0, 2, 1, 3)
        wup[:, G, :, :, 0:nt * 128] = g
        wup[:, G, :, :, 512:512 + nt * 128] = v
    wup = wup.reshape(2, 6, 128, 8 * 1024)
    cw = np.asarray(conv_w, f)
    cbv = np.asarray(conv_b, f)
    cp = np.concatenate([cw, cbv[:, None, :]], axis=1)
    convp = np.ascontiguousarray(cp.reshape(2, 4, 44, 128).transpose(3, 0, 1, 2)).reshape(128, 2 * 4 * 44)
    common = dict(gains=np.ascontiguousarray(gains), wqkv=wqkv, wo=np.ascontiguousarray(np.asarray(w_o, f)), wup=wup,
                  wdn=np.ascontiguousarray(np.asarray(w_down, f)), convp=convp,
                  relT=np.ascontiguousarray(np.asarray(rel_bias, f).T))
    common.update(_constants())
    kl = np.arange(128)[:, None]
    xx = np.arange(256)[None, :]
    dd = xx - kl
    gidx = np.where(dd >= 0, _rel_bucket_np(dd), 32)
    rb_ext = np.concatenate([np.asarray(rel_bias, f), np.full((16, 1), NEGV, f)], axis=1)
    common["gtab"] = np.ascontiguousarray(rb_ext[:, gidx])
    maps = []
    for b in range(8):
        m = dict(common)
        m["xT"] = np.ascontiguousarray(x[b].T)
        maps.append(m)
    return maps


_NC_CACHE = {}


def kernel(x, attn_norm, w_qkv, w_o, rel_bias, ffn_norm, w_up, conv_w, conv_b, w_down, final_norm, _stop_after=None, _nlayers=2, _ncores=8):
    key = (_stop_after, _nlayers)
    if key not in _NC_CACHE:
        _NC_CACHE[key] = build(_nlayers, _stop_after)[0]
    nc = _NC_CACHE[key]
    maps = _layout_inputs(x, attn_norm, w_qkv, w_o, rel_bias, ffn_norm, w_up, conv_w, conv_b, w_down, final_norm)
    if _ncores < 8:
        res = run_bass_kernel_spmd(nc, maps[:_ncores], core_ids=list(range(_ncores)))
        return np.stack([np.asarray(r["outT"]).T for r in res.results], axis=0)
    res = run_bass_kernel_spmd(nc, maps, core_ids=list(range(8)))
    out = np.stack([np.asarray(r["outT"]).T for r in res.results], axis=0)
    return np.ascontiguousarray(out.astype(np.float32))
```
